# Optimizing a Trainium2 kernel written in Bass

```python
import jax, jax.numpy as jnp
from jax import lax
import numpy as np

D_MODEL = 1024
BATCH = 8
SEQ = 2048
DEPTH = 4
DEC_BATCH = 128
DEC_SEQ = 8
PAST_LEN = 16384
PAGE_SIZE = 128

POOL_WINDOWS = (2, 4, 8, 16)
N_POOL_GROUPS = len(POOL_WINDOWS)
POOL_GROUP_DIM = D_MODEL // 8
WIDTH_A = N_POOL_GROUPS * POOL_GROUP_DIM
POOL_BUF = max(POOL_WINDOWS) - 1
CHUNK = 128
N_SGU_HEADS = 8
SGU_HEAD_DIM = D_MODEL // N_SGU_HEADS
WIDTH_B = N_SGU_HEADS * SGU_HEAD_DIM
D_FF = 4 * D_MODEL
D_IN = WIDTH_A + 2 * WIDTH_B + 2 * D_MODEL
EPS = 1e-6

kernel_name = "pool_sgu_gated_hybrid_step"


def rmsnorm(x, g):
    xf = x.astype(jnp.float32)
    y = xf * lax.rsqrt(jnp.mean(xf * xf, axis=-1, keepdims=True) + EPS)
    return (y * g.astype(jnp.float32)).astype(x.dtype)


def layernorm(x, g, b):
    xf = x.astype(jnp.float32)
    mu = jnp.mean(xf, axis=-1, keepdims=True)
    xc = xf - mu
    y = xc * lax.rsqrt(jnp.mean(xc * xc, axis=-1, keepdims=True) + EPS)
    return (y * g.astype(jnp.float32) + b.astype(jnp.float32)).astype(x.dtype)


def pool_mixer(a, buf, start_pos, w_pg, pool_scale):
    bsz, t_len, _ = a.shape
    ext = jnp.concatenate([buf, a], axis=1)
    cs = jnp.cumsum(ext.astype(jnp.float32), axis=1)
    cs = jnp.pad(cs, ((0, 0), (1, 0), (0, 0)))
    pos = start_pos + jnp.arange(t_len)
    hi = cs[:, POOL_BUF + 1:POOL_BUF + 1 + t_len]
    means = []
    for g, w in enumerate(POOL_WINDOWS):
        sl = slice(g * POOL_GROUP_DIM, (g + 1) * POOL_GROUP_DIM)
        lo = cs[:, POOL_BUF + 1 - w:POOL_BUF + 1 - w + t_len, sl]
        cnt = jnp.minimum(w, pos + 1).astype(jnp.float32)[None, :, None]
        means.append((hi[..., sl] - lo) / cnt)
    d = (jnp.concatenate(means, axis=-1) - a.astype(jnp.float32)).astype(a.dtype)
    d = d.reshape(bsz, t_len, N_POOL_GROUPS, POOL_GROUP_DIM)
    y = jnp.einsum('btgc,gcd->btgd', d, w_pg).reshape(bsz, t_len, WIDTH_A) * pool_scale
    return y, ext[:, -POOL_BUF:]


def spatial_gating(u, v, w_s, b_s, ln_g, ln_b):
    bsz, t_len, _ = v.shape
    v = layernorm(v, ln_g, ln_b)
    n_chunks = -(-t_len // CHUNK)
    pad = n_chunks * CHUNK - t_len
    vp = jnp.pad(v, ((0, 0), (0, pad), (0, 0))).reshape(bsz, n_chunks, CHUNK, N_SGU_HEADS, SGU_HEAD_DIM)
    mask = jnp.tril(jnp.ones((CHUNK, CHUNK), dtype=bool))
    wm = jnp.where(mask[None], w_s, jnp.zeros((), w_s.dtype))
    s = jnp.einsum('hts,bnshd->bnthd', wm, vp) + b_s.T[None, None, :, :, None]
    s = s.reshape(bsz, n_chunks * CHUNK, WIDTH_B)[:, :t_len]
    return u * s, v


def layer(x, buf, start_pos, w_in, w_pg, pool_scale, w_s, b_s, ln_g, ln_b,
          w_ba, w_bb, w_out, w_up, w_down, g_pre_mix, g_post_mix, g_pre_ffn, g_post_ffn):
    h = rmsnorm(x, g_pre_mix)
    z = h @ w_in
    c0 = WIDTH_A
    c1 = c0 + WIDTH_B
    c2 = c1 + WIDTH_B
    c3 = c2 + D_MODEL
    a, u, v, ga, gb = z[..., :c0], z[..., c0:c1], z[..., c1:c2], z[..., c2:c3], z[..., c3:]
    ya, new_buf = pool_mixer(a, buf, start_pos, w_pg, pool_scale)
    yb, v_rows = spatial_gating(jax.nn.gelu(u), jax.nn.gelu(v), w_s, b_s, ln_g, ln_b)
    m = jax.nn.sigmoid(ga) * (ya @ w_ba) + jax.nn.sigmoid(gb) * (yb @ w_bb)
    x = x + rmsnorm(m @ w_out, g_post_mix)
    f = jnp.square(jax.nn.relu(rmsnorm(x, g_pre_ffn) @ w_up)) @ w_down
    x = x + rmsnorm(f, g_post_ffn)
    return x, new_buf, v_rows


def setup_inputs(seed: int = 0) -> dict:
    key = jax.random.key(seed)
    ks = jax.random.split(key, 24)
    f32 = jnp.float32

    def nrm(k, shape, scale):
        return jax.random.normal(k, shape, f32) * scale

    return {
        "x_prompt": nrm(ks[0], (BATCH, SEQ, D_MODEL), 1.0),
        "x_sample": nrm(ks[1], (DEC_BATCH, DEC_SEQ, D_MODEL), 1.0),
        "state_pool": nrm(ks[2], (DEPTH, DEC_BATCH, POOL_BUF, WIDTH_A), 1.0),
        "w_in": nrm(ks[3], (DEPTH, D_MODEL, D_IN), D_MODEL ** -0.5),
        "w_pool_grp": nrm(ks[4], (DEPTH, N_POOL_GROUPS, POOL_GROUP_DIM, POOL_GROUP_DIM), POOL_GROUP_DIM ** -0.5),
        "pool_scale": 1.0 + nrm(ks[5], (DEPTH, WIDTH_A), 0.1),
        "w_spatial": nrm(ks[6], (DEPTH, N_SGU_HEADS, CHUNK, CHUNK), 0.1),
        "b_spatial": 1.0 + nrm(ks[7], (DEPTH, N_SGU_HEADS, CHUNK), 0.1),
        "ln_v_g": 1.0 + nrm(ks[8], (DEPTH, WIDTH_B), 0.05),
        "ln_v_b": nrm(ks[9], (DEPTH, WIDTH_B), 0.02),
        "w_branch_a": nrm(ks[10], (DEPTH, WIDTH_A, D_MODEL), WIDTH_A ** -0.5),
        "w_branch_b": nrm(ks[11], (DEPTH, WIDTH_B, D_MODEL), WIDTH_B ** -0.5),
        "w_out": nrm(ks[12], (DEPTH, D_MODEL, D_MODEL), D_MODEL ** -0.5),
        "w_up": nrm(ks[13], (DEPTH, D_MODEL, D_FF), D_MODEL ** -0.5),
        "w_down": nrm(ks[14], (DEPTH, D_FF, D_MODEL), D_FF ** -0.5),
        "g_pre_mix": 1.0 + nrm(ks[15], (DEPTH, D_MODEL), 0.05),
        "g_post_mix": 1.0 + nrm(ks[16], (DEPTH, D_MODEL), 0.05),
        "g_pre_ffn": 1.0 + nrm(ks[17], (DEPTH, D_MODEL), 0.05),
        "g_post_ffn": 1.0 + nrm(ks[18], (DEPTH, D_MODEL), 0.05),
    }


def reference(x_prompt, x_sample, state_pool, w_in, w_pool_grp, pool_scale, w_spatial, b_spatial,
              ln_v_g, ln_v_b, w_branch_a, w_branch_b, w_out, w_up, w_down,
              g_pre_mix, g_post_mix, g_pre_ffn, g_post_ffn):
    xp = x_prompt
    xs = x_sample
    buf_p0 = jnp.zeros((xp.shape[0], POOL_BUF, WIDTH_A), xp.dtype)
    pool_p, pool_s, v_s = [], [], []
    for l in range(DEPTH):
        params = (w_in[l], w_pool_grp[l], pool_scale[l], w_spatial[l], b_spatial[l], ln_v_g[l], ln_v_b[l],
                  w_branch_a[l], w_branch_b[l], w_out[l], w_up[l], w_down[l],
                  g_pre_mix[l], g_post_mix[l], g_pre_ffn[l], g_post_ffn[l])
        xp, bp, _ = layer(xp, buf_p0, 0, *params)
        xs, bs, vs = layer(xs, state_pool[l], PAST_LEN, *params)
        pool_p.append(bp)
        pool_s.append(bs)
        v_s.append(vs)
    state_pool_prompt = jnp.stack(pool_p)
    state_pool_sample = jnp.stack(pool_s)
    state_v_sample = jnp.stack(v_s)
    return (xp, xs, state_pool_prompt, state_pool_sample, state_v_sample)
```

```python
import numpy as np
from contextlib import ExitStack
import concourse.bass as bass
import concourse.mybir as mybir
from concourse.bass_utils import run_bass_kernel_spmd

F32 = mybir.dt.float32
BF16 = mybir.dt.bfloat16
AF = mybir.ActivationFunctionType
ALU = mybir.AluOpType

D = 1024
DIN = 4608
DFF = 4096
DEPTH = 4
NCORES = 8
SEQ = 2048
EPS = 1e-6
TMAX = 768
BLOCKS = [(0, 768, False), (768, 768, False), (1536, 512, True)]
C_A, C_U, C_V, C_GA, C_GB = 0, 512, 1536, 2560, 3584
NSLOT = 4
SLOT_ELEMS = 4096
ROT_BANKS = [0, 1, 2, 3]
DTMP_BANKS = [4, 7]
SS_BANKS = [5, 6]
NRING = 8
SAME_ENGINE_SYNC = True


class Sched:
    def __init__(self, nc, es):
        self.nc = nc
        self.es = es
        self.eng = {"pe": nc.tensor, "act": nc.scalar, "dve": nc.vector, "pool": nc.gpsimd, "sp": nc.sync}
        self.sems = {}
        self.val = {}
        self.seen = {e: {} for e in self.eng}
        self.lw = {}
        self.rd = {}
        for e in ("pe", "act", "dve", "pool"):
            self.add_sem(e)
        self.ring = 0
        for i in range(NRING):
            self.add_sem(("d", i))
        for s in range(NSLOT):
            self.add_sem(("w", s))

    def add_sem(self, key):
        name = "s_" + "_".join(str(k) for k in (key if isinstance(key, tuple) else (key,)))
        self.sems[key] = self.es.enter_context(self.nc.semaphore(name))
        self.val[key] = 0

    def op(self, eng, reads, writes, emit, semkey=None, inc=1, extra=()):
        e = self.eng[eng]
        own = semkey if semkey is not None else eng
        deps = {}

        def add(d):
            if d is None:
                return
            k, v = d
            if k == eng and (eng == "pe" or not SAME_ENGINE_SYNC):
                return
            if deps.get(k, 0) < v:
                deps[k] = v

        for r in reads:
            add(self.lw.get(r))
        for w in writes:
            add(self.lw.get(w))
            for k, v in self.rd.get(w, {}).items():
                add((k, v))
        for d in extra:
            add(d)
        for k, v in deps.items():
            if self.seen[eng].get(k, 0) >= v:
                continue
            e.wait_ge(self.sems[k], v)
            self.seen[eng][k] = v
        inst = emit(e)
        self.val[own] += inc
        inst.then_inc(self.sems[own], inc)
        tick = (own, self.val[own])
        for w in writes:
            self.lw[w] = tick
            self.rd[w] = {}
        for r in reads:
            d = self.rd.setdefault(r, {})
            if d.get(own, 0) < tick[1]:
                d[own] = tick[1]
        return tick

    def dma(self, reads, writes, out, in_):
        i = self.ring
        self.ring = (self.ring + 1) % NRING
        key = ("d", i)
        prev = (key, self.val[key])
        return self.op("sp", reads, writes, lambda e: e.dma_start(out=out, in_=in_), semkey=key, inc=16,
                       extra=(prev,) if prev[1] > 0 else ())

    def finish(self):
        e = self.eng["sp"]
        for i in range(NRING):
            key = ("d", i)
            if self.val[key] > 0:
                e.wait_ge(self.sems[key], self.val[key])


def build():
    nc = bass.Bass("TRN2", target_bir_lowering=False)

    def din(name, shape):
        return nc.dram_tensor(name, shape, F32, kind="ExternalInput").ap()

    def dout(name, shape):
        return nc.dram_tensor(name, shape, F32, kind="ExternalOutput").ap()

    xp = din("xp", [SEQ, D])
    xs = din("xs", [128, D])
    sp_in = din("sp", [DEPTH, 240, 512])
    w_in = din("w_in", [DEPTH, D, DIN])
    w_pg = din("w_pg", [DEPTH, 4, 128, 128])
    pool_scale = din("pool_scale", [DEPTH, 512])
    w_sp = din("w_sp", [DEPTH, 8, 128, 128])
    b_sp = din("b_sp", [DEPTH, 1024])
    ln_g = din("ln_g", [DEPTH, 1024])
    ln_b = din("ln_b", [DEPTH, 1024])
    w_ba = din("w_ba", [DEPTH, 512, D])
    w_bb = din("w_bb", [DEPTH, D, D])
    w_out = din("w_out", [DEPTH, D, D])
    w_up = din("w_up", [DEPTH, D, DFF])
    w_down = din("w_down", [DEPTH, DFF, D])
    g_all = [din(n, [DEPTH, D]) for n in ("g_pre_mix", "g_post_mix", "g_pre_ffn", "g_post_ffn")]
    c_ident = din("c_ident", [128, 128])
    c_tril = din("c_tril", [128, 128])
    c_bmask = din("c_bmask", [128, 128])
    c_rep = din("c_rep", [128, 128])
    c_invcnt = din("c_invcnt", [128, 60])

    yp = dout("yp", [SEQ, D])
    ys = dout("ys", [128, D])
    spp = dout("spp", [DEPTH, 15, 512])
    sps = dout("sps", [DEPTH, 16, 15, 512])
    svs = dout("svs", [DEPTH, 128, D])

    es = ExitStack()
    with es:
        S = Sched(nc, es)

        def sb(name, shape, dt):
            return es.enter_context(nc.sbuf_tensor(name, shape, dt))

        xT = sb("xT", [128, 8, TMAX], F32)
        hT = sb("hT", [128, 8, TMAX], BF16)
        R1 = sb("R1", [128, 6784], F32)
        R2 = sb("R2", [128, 32 * TMAX], BF16)
        AW = 896
        aext = R1[:, 0:4 * AW].rearrange("p (g w) -> p g w", g=4)
        dT = R1[:, 3584:3584 + 1536].bitcast(BF16).rearrange("p (g t) -> p g t", g=4)
        pT = R1[:, 5120:5120 + 1536].bitcast(BF16).rearrange("p (g t) -> p g t", g=4)
        oT = R1[:, 0:8 * TMAX].rearrange("p (c t) -> p c t", c=8)
        B0 = R2[:, 0:8 * TMAX].rearrange("p (c t) -> p c t", c=8)
        B2 = R2[:, 8 * TMAX:16 * TMAX].rearrange("p (c t) -> p c t", c=8)
        B3 = R2[:, 16 * TMAX:24 * TMAX].rearrange("p (i f) -> p i f", f=1024)
        B1 = R2[:, 24 * TMAX:32 * TMAX].rearrange("p (c t) -> p c t", c=8)
        fT = R2[:, :].rearrange("p (c t) -> p c t", c=32)
        sgring = [sb(f"sg{i}", [128, 384], BF16) for i in range(3)]
        sqring = [sb(f"sq{i}", [128, 384], BF16) for i in range(4)]
        tmpring = [sb(f"tmp{i}", [128, 512], F32) for i in range(3)]
        vtm = [sb(f"vtm{i}", [128, 1024], F32) for i in range(2)]
        wslot = [sb(f"wslot{i}", [128, SLOT_ELEMS], BF16) for i in range(NSLOT)]
        pcA = sb("pcA", [128, 128], F32)
        pcB = sb("pcB", [128, 48], F32)
        wpg = sb("wpg", [128, 4, 128], BF16)
        wsp = sb("wsp", [128, 8, 128], F32)
        WmT = sb("WmT", [128, 8, 128], BF16)
        WsT = sb("WsT", [128, 8, 128], BF16)
        Cm = sb("Cm", [128, 8, 128], F32)
        Cs = sb("Cs", [128, 8, 128], F32)
        bb16 = sb("bb16", [128, 1024], BF16)
        bsb = sb("bsb", [128, 8, 128], F32)
        gb32 = sb("gb32", [128, 1024], F32)
        bb32 = sb("bb32", [128, 1024], F32)
        ident = sb("ident", [128, 128], F32)
        ones16 = sb("ones16", [128, 128], BF16)
        trilT = sb("trilT", [128, 128], F32)
        bmask = sb("bmask", [128, 128], F32)
        rep32 = sb("rep32", [128, 128], F32)
        rep16 = sb("rep16", [128, 128], BF16)
        invcnt = sb("invcnt", [128, 4, 15], F32)
        atail = sb("atail", [128, DEPTH, 4, 15], F32)
        anew = sb("anew", [128, 4, 128], F32)
        t3 = sb("t3", [128, 16], F32)
        t3p = sb("t3p", [128, 16], F32)
        ptmp = [sb(f"ptmp{i}", [128, 400], F32) for i in range(2)]
        stats = [sb(f"stats{i}", [128, 2, 6], F32) for i in range(2)]
        mv = [sb(f"mv{i}", [128, 2], F32) for i in range(2)]
        sm = [sb(f"sm{i}", [128, 4], F32) for i in range(2)]
        pstage = sb("pstage", [128, 176], F32)
        vstats = sb("vstats", [128, 6, 2, 6], F32)
        vmv = sb("vmv", [128, 6, 2], F32)
        vrs = sb("vrs", [128, 8], F32)
        vnm = sb("vnm", [128, 8], F32)
        ps = [es.enter_context(nc.psum_tensor(f"ps{i}", [128, 512], F32)) for i in range(8)]

        st = {"rot": 0, "sq": 0, "sg": 0, "tmp": 0, "vtm": 0, "slot": 0, "small": 0, "alt": 0, "dtmp": 0}

        def nxt(name, n):
            v = st[name]
            st[name] = (v + 1) % n
            return v

        def bank():
            return ROT_BANKS[nxt("rot", len(ROT_BANKS))]

        def alt_eng():
            st["alt"] ^= 1
            return "act" if st["alt"] else "dve"

        def copy_op(eng, reads, writes, out, in_):
            if eng == "act":
                return S.op("act", reads, writes, lambda e: e.activation(out=out, in_=in_, func=AF.Copy))
            return S.op("dve", reads, writes, lambda e: e.tensor_copy(out=out, in_=in_))

        S.dma([], ["ident"], ident[:], c_ident)
        S.dma([], ["trilT"], trilT[:], c_tril)
        S.dma([], ["bmask"], bmask[:], c_bmask)
        S.dma([], ["rep32"], rep32[:], c_rep)
        S.dma([], ["invcnt"], invcnt[:].rearrange("p g w -> p (g w)"), c_invcnt)
        S.op("dve", [], ["ones16"], lambda e: e.memset(ones16[:], 1.0 / 1024.0))
        S.op("dve", ["rep32"], ["rep16"], lambda e: e.tensor_copy(out=rep16[:], in_=rep32[:]))
        for k in range(4):
            S.dma([], [("pstage", k)], pstage[32 * k:32 * (k + 1), 0:128], g_all[k].rearrange("l (c p) -> (l c) p", p=128))
        pstB = sb("pstB", [128, 128], F32)
        S.dma([], ["pstB"], pstB[0:32, :], ln_g.rearrange("l (c p) -> (l c) p", p=128))
        S.dma([], ["pstB2"], pstB[32:48, :], pool_scale.rearrange("l (c p) -> (l c) p", p=128))
        b = bank()
        S.op("pe", [("pstage", k) for k in range(4)] + ["ident"], [("ps", b)],
             lambda e: e.transpose(out=ps[b][:, 0:128], in_=pstage[:, 0:128], identity=ident[:]))
        S.op("dve", [("ps", b)], ["pcA"], lambda e: e.tensor_copy(out=pcA[:], in_=ps[b][:, 0:128]))
        b = bank()
        S.op("pe", ["pstB", "pstB2", "ident"], [("ps", b)],
             lambda e: e.transpose(out=ps[b][:, 0:48], in_=pstB[0:48, :], identity=ident[0:48, 0:48]))
        S.op("dve", [("ps", b)], ["pcB"], lambda e: e.tensor_copy(out=pcB[:], in_=ps[b][:, 0:48]))

        def gcol(kind, l, c):
            return pcA[:, kind * 32 + l * 8 + c: kind * 32 + l * 8 + c + 1]

        def lngcol(l, c):
            return pcB[:, l * 8 + c: l * 8 + c + 1]

        def pscol(l, g):
            return pcB[:, 32 + l * 4 + g: 32 + l * 4 + g + 1]

        def load_slab(src_ap, kc, ncols):
            s = nxt("slot", NSLOT)
            dst = wslot[s][:, 0:kc * ncols].rearrange("p (k n) -> p k n", k=kc)
            S.op("pool", [], [("wslot", s)], lambda e: e.dma_start(out=dst, in_=src_ap), semkey=("w", s), inc=16)
            return s, dst

        def wview(w2d, c0, ncols):
            return w2d.rearrange("(k p) n -> p k n", p=128)[:, :, c0:c0 + ncols]

        def param_dmas(l, samp):
            S.dma([], ["wsp"], wsp[:], w_sp[l].rearrange("h t s -> t h s"))
            S.dma([], ["bsb"], bsb[:].rearrange("p h t -> p (h t)"), b_sp[l:l + 1, :].broadcast_to([128, 1024]))
            if samp:
                S.dma([], ["gb32"], gb32[:], ln_g[l:l + 1, :].broadcast_to([128, 1024]))
                S.dma([], ["bb32"], bb32[:], ln_b[l:l + 1, :].broadcast_to([128, 1024]))

        wpg32 = sb("wpg32", [128, 4, 128], F32)

        def param_dmas2(l):
            S.dma([], ["wpg32"], wpg32[:], w_pg[l].rearrange("g c d -> c g d"))

        def layer_prep(l, samp, phases=(0, 1, 2)):
            if 0 in phases:
                S.op("dve", ["wpg32"], ["wpg"], lambda e: e.tensor_copy(out=wpg[:], in_=wpg32[:]))
            for h0 in ((0, 4) if 0 in phases else ()):
                b = bank()
                for hh in range(4):
                    h = h0 + hh
                    S.op("pe", ["wsp", "ident"], [("ps", b)] if hh == 0 else [("psx", b, hh)],
                         lambda e, h=h, hh=hh, b=b: e.transpose(out=ps[b][:, hh * 128:(hh + 1) * 128], in_=wsp[:, h, :],
                                                                identity=ident[:]))
                S.op("dve", [("ps", b), ("psx", b, 1), ("psx", b, 2), ("psx", b, 3), "trilT"], [("WmT", h0)],
                     lambda e, b=b, h0=h0: e.tensor_tensor(
                         out=WmT[:, h0:h0 + 4, :], in0=ps[b][:, :].rearrange("p (a t) -> p a t", a=4),
                         in1=trilT[:].unsqueeze(1).broadcast_to([128, 4, 128]), op=ALU.mult))
            if samp and 1 in phases:
                for h0 in (0, 4):
                    b = bank()
                    for hh in range(4):
                        h = h0 + hh
                        S.op("pe", [("WmT", h0), "rep16"], [("ps", b)] if hh == 0 else [("psx", b, hh)],
                             lambda e, h=h, hh=hh, b=b: e.matmul(
                                 ps[b][:, hh * 128:(hh + 1) * 128], lhsT=rep16[:, :],
                                 rhs=WmT[:, h, 0:8].unsqueeze(1).broadcast_to([128, 16, 8]), start=True, stop=True))
                    S.op("dve", [("ps", b), ("psx", b, 1), ("psx", b, 2), ("psx", b, 3), "bmask"], [("WsT", h0)],
                         lambda e, b=b, h0=h0: e.tensor_tensor(
                             out=WsT[:, h0:h0 + 4, :], in0=ps[b][:, :].rearrange("p (a t) -> p a t", a=4),
                             in1=bmask[:].unsqueeze(1).broadcast_to([128, 4, 128]), op=ALU.mult))
            for (Wt, Ct, key, ckey) in (([(WmT, Cm, "WmT", "Cm")] + ([(WsT, Cs, "WsT", "Cs")] if samp else []))
                                        if 2 in phases else []):
                for h0 in (0, 4):
                    b = bank()
                    for hh in range(4):
                        h = h0 + hh
                        S.op("pe", [(key, h0), "bb16"], [("ps", b)] if hh == 0 else [("psx", b, hh)],
                             lambda e, h=h, hh=hh, b=b, Wt=Wt: e.matmul(
                                 ps[b][:, hh * 128:(hh + 1) * 128], lhsT=bb16[:, h * 128:(h + 1) * 128],
                                 rhs=Wt[:, h, :], start=True, stop=True))
                    if ckey == "Cm":
                        S.op("dve", [("ps", b), ("psx", b, 1), ("psx", b, 2), ("psx", b, 3), "bsb"], [(ckey, h0)],
                             lambda e, b=b, h0=h0, Ct=Ct: e.tensor_tensor(
                                 out=Ct[:, h0:h0 + 4, :], in0=ps[b][:, :].rearrange("p (a t) -> p a t", a=4),
                                 in1=bsb[:, h0:h0 + 4, :], op=ALU.add))
                    else:
                        for hh in range(4):
                            S.op("dve", [("ps", b), ("psx", b, 1), ("psx", b, 2), ("psx", b, 3), "bsb"], [(ckey, h0)],
                                 lambda e, b=b, h0=h0, hh=hh, Ct=Ct: e.tensor_tensor(
                                     out=Ct[:, h0 + hh, :].rearrange("p (q r) -> p q r", r=8),
                                     in0=ps[b][:, hh * 128:(hh + 1) * 128].rearrange("p (q r) -> p q r", r=8),
                                     in1=bsb[:, h0 + hh, 0:8].unsqueeze(1).broadcast_to([128, 16, 8]), op=ALU.add))

        def norm_stats_finish(n, N, corr=False):
            bk = SS_BANKS[n]
            if corr:
                S.op("dve", [("ss", n), ("ptmp", n)], [("ss", n)],
                     lambda e: e.tensor_tensor(out=ps[bk][:, 0:N], in0=ps[bk][:, 0:N], in1=ptmp[n][:, 0:N], op=ALU.add))
            S.op("act", [("ss", n)], [("ss", n)],
                 lambda e: e.activation(out=ps[bk][:, 0:N], in_=ps[bk][:, 0:N], func=AF.Ln,
                                        bias=(0.0 if corr else EPS), scale=1.0))
            S.op("act", [("ss", n)], [("ss", n)],
                 lambda e: e.activation(out=ps[bk][:, 0:N], in_=ps[bk][:, 0:N], func=AF.Exp, scale=-0.5))

        def pre_norm_sub(l, kind, n, n0, N):
            bk = SS_BANKS[n]
            for c in range(8):
                q = nxt("sq", 4)
                S.op("act", [("xT", c, n)], [("sq", q)],
                     lambda e, c=c, q=q: e.activation(out=sqring[q][:, 0:N], in_=xT[:, c, n0:n0 + N], func=AF.Square))
                S.op("pe", [("sq", q), "ones16"], [("ss", n)],
                     lambda e, c=c, q=q: e.matmul(ps[bk][:, 0:N], lhsT=ones16[:], rhs=sqring[q][:, 0:N],
                                                  start=(c == 0), stop=(c == 7)))
            norm_stats_finish(n, N)
            for c in range(8):
                S.op("dve", [("xT", c, n), ("ss", n), "pcA"], [("hT", c, n)],
                     lambda e, c=c: e.scalar_tensor_tensor(out=hT[:, c, n0:n0 + N], in0=xT[:, c, n0:n0 + N],
                                                           scalar=gcol(kind, l, c), in1=ps[bk][:, 0:N],
                                                           op0=ALU.mult, op1=ALU.mult))

        def pre_norm(l, kind, subs):
            for n, (n0, N) in enumerate(subs):
                pre_norm_sub(l, kind, n, n0, N)

        def pre_norm_sq(n, n0, N):
            for c in range(8):
                S.op("act", [("xT", c, n)], [("hT", c, n)],
                     lambda e, c=c: e.activation(out=hT[:, c, n0:n0 + N], in_=xT[:, c, n0:n0 + N], func=AF.Square))

        def pre_norm_fin(l, kind, n, n0, N):
            bk = SS_BANKS[n]
            for c in range(8):
                S.op("pe", [("hT", c, n), "ones16"], [("ss", n)],
                     lambda e, c=c: e.matmul(ps[bk][:, 0:N], lhsT=ones16[:], rhs=hT[:, c, n0:n0 + N],
                                             start=(c == 0), stop=(c == 7)))
            norm_stats_finish(n, N)
            for c in range(8):
                S.op("dve", [("xT", c, n), ("ss", n), "pcA"], [("hT", c, n)],
                     lambda e, c=c: e.scalar_tensor_tensor(out=hT[:, c, n0:n0 + N], in0=xT[:, c, n0:n0 + N],
                                                           scalar=gcol(kind, l, c), in1=ps[bk][:, 0:N],
                                                           op0=ALU.mult, op1=ALU.mult))

        def proj_group(kc, lhs_fn, rhs_fn, reads, n0, N):
            b = bank()

            def emit(e):
                last = None
                for k in range(kc):
                    last = e.matmul(ps[b][:, 0:N], lhsT=lhs_fn(k), rhs=rhs_fn(k), start=(k == 0), stop=(k == kc - 1))
                return last
            S.op("pe", reads, [("ps", b)], emit)
            return b

        def post_norm_sub(l, kind, n, n0, N, corr=False):
            bk = SS_BANKS[n]
            norm_stats_finish(n, N, corr)
            pend = []

            def do_add(c, t):
                S.op("dve", [("ps", t), ("xT", c, n)], [("xT", c, n)],
                     lambda e: e.tensor_tensor(out=xT[:, c, n0:n0 + N], in0=xT[:, c, n0:n0 + N],
                                               in1=ps[t][:, 0:N], op=ALU.add))
            for c in range(8):
                t = DTMP_BANKS[nxt("dtmp", 2)]
                S.op("dve", [("oT", c, n), ("R1", n), ("ss", n), "pcA"], [("ps", t)],
                     lambda e, c=c, t=t: e.scalar_tensor_tensor(out=ps[t][:, 0:N], in0=oT[:, c, n0:n0 + N],
                                                                scalar=gcol(kind, l, c), in1=ps[bk][:, 0:N],
                                                                op0=ALU.mult, op1=ALU.mult))
                pend.append((c, t))
                if len(pend) > 1:
                    do_add(*pend.pop(0))
            while pend:
                do_add(*pend.pop(0))

        def norm_boundary(l_post, kind_post, pre, subs, corr=False):
            for n, (n0, N) in enumerate(subs):
                post_norm_sub(l_post, kind_post, n, n0, N, corr)
                if pre is not None:
                    pre_norm_sub(pre[0], pre[1], n, n0, N)

        def ffn_boundary_sub(l, n, n0, N):
            post_norm_sub(l, 1, n, n0, N)
            ffn_h(l, n, n0, N)

        def ffn_h(l, n, n0, N):
            for c in range(8):
                S.op("act", [("xT", c, n), "pcA"], [("hT", c, n)],
                     lambda e, c=c: e.activation(out=hT[:, c, n0:n0 + N], in_=xT[:, c, n0:n0 + N], func=AF.Copy,
                                                 scale=gcol(2, l, c)))

        def ffn_stats_fn(l, subs):
            all_steps = []
            for n, (n0, N) in enumerate(subs):
                bk = SS_BANKS[n]
                slots = {}

                def mk_sq(c, n=n, n0=n0, N=N, slots=slots):
                    def f():
                        q = nxt("sq", 4)
                        slots[c] = q
                        S.op("act", [("xT", c, n)], [("sq", q)],
                             lambda e: e.activation(out=sqring[q][:, 0:N], in_=xT[:, c, n0:n0 + N], func=AF.Square))
                    return f

                def mk_mm(c, n=n, N=N, bk=bk, slots=slots):
                    def f():
                        q = slots[c]
                        S.op("pe", [("sq", q), "ones16"], [("ss", n)],
                             lambda e: e.matmul(ps[bk][:, 0:N], lhsT=ones16[:], rhs=sqring[q][:, 0:N],
                                                start=(c == 0), stop=(c == 7)))
                        if c == 7:
                            S.op("act", [("ss", n)], [("ptmp", n)],
                                 lambda e: e.activation(out=ptmp[n][:, 0:N], in_=ps[bk][:, 0:N], func=AF.Square,
                                                        scale=EPS ** 0.5, bias=EPS ** 1.5))
                    return f
                steps = []
                prev = None
                for c in range(8):
                    steps.append((mk_sq(c), prev))
                    prev = mk_mm(c)
                steps.append((None, prev))
                all_steps.append(steps)
            return all_steps

        def out_proj_with_stats(kc, in_buf, in_key, coarse, slabs_fn, subs, tail=0, mid_cb=None, late_cb=None):
            pend = []

            def do_ss(q, j, n, N):
                bk = SS_BANKS[n]
                S.op("pe", [("sq", q), "ones16"], [("ss", n)],
                     lambda e: e.matmul(ps[bk][:, 0:N], lhsT=ones16[:], rhs=sqring[q][:, 0:N],
                                        start=(j == 0), stop=(j == 7)))

            def group(j, n, n0, N, depth):
                sl, sv, cj = slabs_fn(j)
                b = proj_group(kc, lambda k: sv[:, k, cj:cj + 128], lambda k: in_buf[:, k, n0:n0 + N],
                               [("wslot", sl)] + list(coarse) + [(in_key, k, n) for k in range(kc)], n0, N)
                S.op("act", [("ps", b)], [("oT", j, n), ("R1", n)],
                     lambda e: e.activation(out=oT[:, j, n0:n0 + N], in_=ps[b][:, 0:N], func=AF.Copy))
                q = nxt("sq", 4)
                S.op("act", [("ps", b)], [("sq", q)],
                     lambda e: e.activation(out=sqring[q][:, 0:N], in_=ps[b][:, 0:N], func=AF.Square))
                pend.append((q, j, n, N))
                while len(pend) > depth:
                    do_ss(*pend.pop(0))
            head = 8 - tail
            for j in range(head):
                for n, (n0, N) in enumerate(subs):
                    group(j, n, n0, N, 2)
            if tail == 0:
                while pend:
                    do_ss(*pend.pop(0))
                if mid_cb is not None:
                    mid_cb()
                return
            for n, (n0, N) in enumerate(subs):
                for j in range(head, 8):
                    if n == 1 and j == 7 and late_cb is not None:
                        late_cb()
                    group(j, n, n0, N, 2)
                while pend:
                    do_ss(*pend.pop(0))
                if n == 0 and mid_cb is not None:
                    mid_cb()

        first = True
        for bi, (p0, npr, samp) in enumerate(BLOCKS):
            T = npr + (128 if samp else 0)
            NT = T // 128
            subs = [(0, 384), (384, T - 384)]
            SBASE = 15 + npr

            def tile_sub(i):
                return 0 if i < 3 else 1

            for i in range(NT):
                v = nxt("vtm", 2)
                src = xs if (samp and i == NT - 1) else xp[p0 + 128 * i: p0 + 128 * (i + 1), :]
                S.dma([], [("vtm", v)], vtm[v][:], src)
                n = tile_sub(i)
                for half in range(2):
                    b = bank()
                    for cc in range(4):
                        c = half * 4 + cc
                        S.op("pe", [("vtm", v), "ident"], [("ps", b)] if cc == 0 else [("psx", b, cc)],
                             lambda e, b=b, c=c, cc=cc, v=v: e.transpose(out=ps[b][:, cc * 128:(cc + 1) * 128],
                                                                         in_=vtm[v][:, c * 128:(c + 1) * 128],
                                                                         identity=ident[:]))
                    copy_op(alt_eng(), [("ps", b), ("psx", b, 1), ("psx", b, 2), ("psx", b, 3)],
                            [("xT", half * 4 + cc, n) for cc in range(4)],
                            xT[:, half * 4:half * 4 + 4, i * 128:(i + 1) * 128],
                            ps[b][:, :].rearrange("p (a t) -> p a t", a=4))

            def all_param_dmas(l2, samp2):
                param_dmas(l2, samp2)
                param_dmas2(l2)
                if not samp2:
                    S.dma([], ["bb32"], bb32[:], ln_b[l2:l2 + 1, :].broadcast_to([128, 1024]))

            def prep_compute(l2, samp2, phases=(0, 1, 2)):
                if 0 in phases:
                    S.op("dve", ["bb32"], ["bb16"], lambda e: e.tensor_copy(out=bb16[:], in_=bb32[:]))
                layer_prep(l2, samp2, phases)

            def next_layer(bi2, l2):
                if l2 + 1 < DEPTH:
                    return bi2, l2 + 1
                if bi2 + 1 < len(BLOCKS):
                    return bi2 + 1, 0
                return None

            deferred_fin = []
            for l in range(DEPTH):
                w_in_l = w_in[l]
                if bi == 0 and l == 0:
                    all_param_dmas(0, samp)
                    prep_compute(0, samp)
                if l == 0:
                    pre_norm(l, 0, subs)

                if bi == 0:
                    S.op("dve", [], [("aext", g) for g in range(4)] + [("R1", 0), ("R1", 1)],
                         lambda e: e.memset(aext[:, :, 0:15], 0.0))
                else:
                    S.op("dve", [("atail", l)], [("aext", g) for g in range(4)] + [("R1", 0), ("R1", 1)],
                         lambda e, l=l: e.tensor_copy(out=aext[:, :, 0:15], in_=atail[:, l, :, :]))
                if samp:
                    v = nxt("vtm", 2)
                    stg = vtm[v][:, :].rearrange("p (a f) -> p a f", a=2)
                    for hh in range(2):
                        S.dma([], [("vtm", v)] if hh == 0 else [("vtmx", v)], stg[0:120, hh, :],
                              sp_in[l, hh * 120:(hh + 1) * 120, :])
                    for hh in range(2):
                        b = bank()
                        for g in range(4):
                            S.op("pe", [("vtm", v), ("vtmx", v), "ident"], [("ps", b)] if g == 0 else [("psx", b, g)],
                                 lambda e, b=b, g=g, hh=hh, stg=stg: e.transpose(
                                     out=ps[b][:, g * 128:g * 128 + 120], in_=stg[0:120, hh, g * 128:(g + 1) * 128],
                                     identity=ident[0:120, 0:120]))
                        for g in range(4):
                            dst = aext[:, g, SBASE + hh * 8 * 23: SBASE + (hh + 1) * 8 * 23].rearrange(
                                "p (q r) -> p q r", r=23)[:, :, 0:15]
                            srcp = ps[b][:, g * 128:g * 128 + 120].rearrange("p (q r) -> p q r", r=15)
                            copy_op(alt_eng(), [("ps", b), ("psx", b, 1), ("psx", b, 2), ("psx", b, 3)],
                                    [("aext", g), ("R1", 0), ("R1", 1)], dst, srcp)
                    S.dma([], [("sps_old", l)], sps[l, :, 0:7, :],
                          sp_in[l].rearrange("(q r) f -> q r f", r=15)[:, 8:15, :])

                sl_a, sv_a = load_slab(wview(w_in_l, C_A, 512), 8, 512)
                slu = [load_slab(wview(w_in_l, C_U + 512 * half, 512), 8, 512) for half in range(2)]

                def a_stage(n, n0, N):
                    sl, sv = sl_a, sv_a
                    for j in range(4):
                        b = proj_group(8, lambda k: sv[:, k, j * 128:(j + 1) * 128], lambda k: hT[:, k, n0:n0 + N],
                                       [("wslot", sl)] + [("hT", k, n) for k in range(8)], n0, N)
                        pe_ = min(n0 + N, npr)
                        if pe_ > n0:
                            S.op("act", [("ps", b)], [("aext", j), ("R1", 0), ("R1", 1)],
                                 lambda e, b=b, j=j, pe_=pe_: e.activation(out=aext[:, j, 15 + n0:15 + pe_],
                                                                          in_=ps[b][:, 0:pe_ - n0], func=AF.Copy))
                        if samp and n0 + N > npr:
                            o0 = npr - n0
                            dst = aext[:, j, SBASE:SBASE + 368].rearrange("p (q r) -> p q r", r=23)[:, :, 15:23]
                            S.op("act", [("ps", b)], [("aext", j), ("R1", 0), ("R1", 1)],
                                 lambda e, b=b, dst=dst, o0=o0: e.activation(
                                     out=dst, in_=ps[b][:, o0:o0 + 128].rearrange("p (q r) -> p q r", r=8), func=AF.Copy))
                            S.op("act", [("ps", b)], [("anew", j)],
                                 lambda e, b=b, j=j, o0=o0: e.activation(out=anew[:, j, :], in_=ps[b][:, o0:o0 + 128],
                                                                         func=AF.Copy))

                def u_stage(n, n0, N):
                    for half in range(2):
                        sl, sv = slu[half]
                        for hh in range(4):
                            h = half * 4 + hh
                            b = proj_group(8, lambda k: sv[:, k, hh * 128:(hh + 1) * 128], lambda k: hT[:, k, n0:n0 + N],
                                           [("wslot", sl)] + [("hT", k, n) for k in range(8)], n0, N)
                            S.op("act", [("ps", b)], [("B0", h, n), "R2a"],
                                 lambda e, b=b, h=h: e.activation(out=B0[:, h, n0:n0 + N], in_=ps[b][:, 0:N],
                                                                  func=AF.Gelu_apprx_tanh))

                def a_outputs():
                    if bi < len(BLOCKS) - 1:
                        S.op("act", [("aext", g) for g in range(4)], [("atail", l)],
                             lambda e, l=l: e.activation(out=atail[:, l, :, :], in_=aext[:, :, npr:npr + 15], func=AF.Copy))
                    if samp:
                        b = bank()
                        for g in range(4):
                            S.op("pe", [("aext", g), "ident"], [("ps", b)] if g == 0 else [("psx", b, g)],
                                 lambda e, b=b, g=g: e.transpose(out=ps[b][0:15, g * 128:(g + 1) * 128],
                                                                 in_=aext[:, g, npr:npr + 15], identity=ident[:]))
                        v = nxt("vtm", 2)
                        copy_op("dve", [("ps", b), ("psx", b, 1), ("psx", b, 2), ("psx", b, 3)], [("vtm", v)],
                                vtm[v][0:15, 0:512], ps[b][0:15, :])
                        S.dma([("vtm", v)], [("spp", l)], spp[l], vtm[v][0:15, 0:512])
                        b = bank()
                        for g in range(4):
                            S.op("pe", [("anew", g), "ident"], [("ps", b)] if g == 0 else [("psx", b, g)],
                                 lambda e, b=b, g=g: e.transpose(out=ps[b][:, g * 128:(g + 1) * 128], in_=anew[:, g, :],
                                                                 identity=ident[:]))
                        copy_op("dve", [("ps", b), ("psx", b, 1), ("psx", b, 2), ("psx", b, 3)], [("vtmx", v)],
                                vtm[v][:, 512:1024], ps[b][:, :])
                        for q in range(16):
                            S.dma([("vtmx", v)], [("sps_new", l, q)], sps[l, q, 7:15, :], vtm[v][q * 8:(q + 1) * 8, 512:1024])

                def pool_chain(g, seg, ta, tb, tk, t3x, t3k):
                    kind, c0, L, n = seg
                    w = 2 << g
                    ops = []
                    if kind == "p":
                        def E(off, length):
                            return aext[:, g, 15 + c0 + off: 15 + c0 + off + length]

                        def Tm(t, off, length):
                            return t[:, 15 + off: 15 + off + length]
                        outv = dT[:, g, c0:c0 + L]
                    else:
                        def E(off, length):
                            return aext[:, g, SBASE:SBASE + 368].rearrange("p (q r) -> p q r", r=23)[
                                :, :, 15 + off:15 + off + length]

                        def Tm(t, off, length):
                            return t[:, 0:368].rearrange("p (q r) -> p q r", r=23)[:, :, 15 + off:15 + off + length]
                        outv = dT[:, g, npr:npr + 128].rearrange("p (q r) -> p q r", r=8)
                    keys_r = [("aext", g), ("R1", 0), ("R1", 1)]
                    lo = -14
                    ops.append(lambda lo=lo: S.op("dve", keys_r, [tk[0]],
                               lambda e: e.tensor_tensor(out=Tm(ta, lo, L - lo), in0=E(lo, L - lo), in1=E(lo - 1, L - lo),
                                                         op=ALU.add)))
                    cur, oth, ci = ta, tb, 0
                    sh = 2
                    while sh < w:
                        lo2 = lo + sh
                        ops.append(lambda cur=cur, oth=oth, ci=ci, lo2=lo2, sh=sh: S.op(
                            "dve", [tk[ci]], [tk[1 - ci]],
                            lambda e: e.tensor_tensor(out=Tm(oth, lo2, L - lo2), in0=Tm(cur, lo2, L - lo2),
                                                      in1=Tm(cur, lo2 - sh, L - lo2), op=ALU.add)))
                        lo = lo2
                        cur, oth = oth, cur
                        ci = 1 - ci
                        sh *= 2
                    fix = (bi == 0 and kind == "p" and c0 == 0)
                    if fix:
                        ops.append(lambda cur=cur, ci=ci: S.op(
                            "dve", [tk[ci], "invcnt"], [t3k],
                            lambda e: e.tensor_tensor(out=t3x[:, 0:15], in0=Tm(cur, 0, 15), in1=invcnt[:, g, :],
                                                      op=ALU.mult)))
                    ops.append(lambda cur=cur, ci=ci: S.op(
                        "dve", [tk[ci]] + keys_r, [("dT", g, n), ("R1", 0), ("R1", 1)],
                        lambda e: e.scalar_tensor_tensor(out=outv, in0=Tm(cur, 0, L), scalar=1.0 / w, in1=E(0, L),
                                                         op0=ALU.mult, op1=ALU.subtract)))
                    if fix:
                        ops.append(lambda: S.op(
                            "dve", [t3k] + keys_r, [("dT", g, n), ("R1", 0), ("R1", 1)],
                            lambda e: e.tensor_tensor(out=dT[:, g, 0:15], in0=t3x[:, 0:15], in1=E(0, 15),
                                                      op=ALU.subtract)))
                    return ops

                def pooling(seg):
                    for (gA, gB) in ((3, 2), (1, 0)):
                        A = pool_chain(gA, seg, tmpring[0], tmpring[1], [("tmp", 0), ("tmp", 1)], t3, "t3")
                        Bc = pool_chain(gB, seg, ptmp[0], ptmp[1], [("ptmp", 0), ("ptmp", 1)], t3p, "t3p")
                        for i in range(max(len(A), len(Bc))):
                            if i < len(A):
                                A[i]()
                            if i < len(Bc):
                                Bc[i]()

                for n, (n0, N) in enumerate(subs):
                    if n == 1:
                        while deferred_fin:
                            deferred_fin.pop(0)()
                    a_stage(n, n0, N)
                    if n == len(subs) - 1:
                        a_outputs()
                    pe_ = min(n0 + N, npr)
                    if pe_ > n0:
                        pooling(("p", n0, pe_ - n0, n))
                    if samp and n == len(subs) - 1:
                        pooling(("s", 0, 8, 1))
                    u_stage(n, n0, N)

                slv = [load_slab(wview(w_in_l, C_V + 512 * half, 512), 8, 512) for half in range(2)]
                vsamp = None
                for i in range(NT):
                    n = tile_sub(i)
                    issamp = samp and i == NT - 1
                    if issamp:
                        vsamp = nxt("vtm", 2)
                    for half in range(2):
                        slh, svh = slv[half]
                        b = proj_group(8, lambda k: hT[:, k, i * 128:(i + 1) * 128], lambda k: svh[:, k, :],
                                       [("wslot", slh)] + [("hT", k, n) for k in range(8)], 0, 512)
                        S.op("act", [("ps", b)], [("B3", i, half), "R2c"],
                             lambda e: e.activation(out=B3[:, i, half * 512:(half + 1) * 512], in_=ps[b][:, 0:512],
                                                    func=AF.Gelu_apprx_tanh))
                        if issamp:
                            S.op("act", [("ps", b)], [("vtm", vsamp)] if half == 0 else [("vtmx", vsamp)],
                                 lambda e: e.activation(out=vtm[vsamp][:, half * 512:(half + 1) * 512],
                                                        in_=ps[b][:, 0:512], func=AF.Gelu_apprx_tanh))
                            S.op("dve", [("vtm", vsamp), ("vtmx", vsamp)], [("vstats", i, half)],
                                 lambda e: e.bn_stats(out=vstats[:, i, half, :],
                                                      in_=vtm[vsamp][:, half * 512:(half + 1) * 512]))
                        else:
                            S.op("dve", [("B3", i, half)], [("vstats", i, half)],
                                 lambda e: e.bn_stats(out=vstats[:, i, half, :], in_=B3[:, i, half * 512:(half + 1) * 512]))
                def v_finish():
                    for i in range(NT):
                        S.op("dve", [("vstats", i, 0), ("vstats", i, 1)], [("vmv", i)],
                             lambda e: e.bn_aggr(out=vmv[:, i, :], in_=vstats[:, i, :, :].rearrange("p a s -> p (a s)")))
                    S.op("act", [("vmv", i) for i in range(NT)], ["vrs"],
                         lambda e: e.activation(out=vrs[:, 0:NT], in_=vmv[:, 0:NT, 1], func=AF.Ln, bias=EPS, scale=1.0))
                    S.op("act", ["vrs"], ["vrs"],
                         lambda e: e.activation(out=vrs[:, 0:NT], in_=vrs[:, 0:NT], func=AF.Exp, scale=-0.5))
                    S.op("dve", ["vrs"] + [("vmv", i) for i in range(NT)], ["vnm"],
                         lambda e: e.scalar_tensor_tensor(out=vnm[:, 0:NT], in0=vmv[:, 0:NT, 0], scalar=-1.0, in1=vrs[:, 0:NT],
                                                          op0=ALU.mult, op1=ALU.mult))
                    for i in range(NT):
                        issamp = samp and i == NT - 1
                        if issamp:
                            v = vsamp
                            S.op("act", [("vtm", v), ("vtmx", v), "vrs", "vnm"], [("B3", i, 0), ("B3", i, 1), "R2c"],
                                 lambda e: e.activation(out=B3[:, i, :], in_=vtm[v][:, :], func=AF.Identity,
                                                        scale=vrs[:, i:i + 1], bias=vnm[:, i:i + 1]))
                            S.op("act", [("vtm", v), ("vtmx", v), "vrs", "vnm"], [("vtm", v), ("vtmx", v)],
                                 lambda e: e.activation(out=vtm[v][:, :], in_=vtm[v][:, :], func=AF.Identity,
                                                        scale=vrs[:, i:i + 1], bias=vnm[:, i:i + 1]))
                            S.op("dve", [("vtm", v), ("vtmx", v), "gb32"], [("vtm", v), ("vtmx", v)],
                                 lambda e: e.tensor_tensor(out=vtm[v][:, :], in0=vtm[v][:, :], in1=gb32[:], op=ALU.mult))
                            S.op("dve", [("vtm", v), ("vtmx", v), "bb32"], [("vtm", v), ("vtmx", v)],
                                 lambda e: e.tensor_tensor(out=vtm[v][:, :], in0=vtm[v][:, :], in1=bb32[:], op=ALU.add))
                            S.dma([("vtm", v), ("vtmx", v)], [("svs", l)], svs[l], vtm[v][:, :])
                        else:
                            S.op("dve", [("B3", i, 0), ("B3", i, 1), "vrs", "vnm"], [("B3", i, 0), ("B3", i, 1), "R2c"],
                                 lambda e: e.tensor_scalar(out=B3[:, i, :], in0=B3[:, i, :], scalar1=vrs[:, i:i + 1],
                                                           scalar2=vnm[:, i:i + 1], op0=ALU.mult, op1=ALU.add))

                for n, (n0, N) in enumerate(subs):
                    for g in (3, 2, 1, 0):
                        b = proj_group(1, lambda k: wpg[:, g, :], lambda k: dT[:, g, n0:n0 + N],
                                       ["wpg", ("dT", g, n)], n0, N)
                        S.op("act", [("ps", b), "pcB"], [("pT", g, n)],
                             lambda e, b=b, g=g: e.activation(out=pT[:, g, n0:n0 + N], in_=ps[b][:, 0:N], func=AF.Copy,
                                                              scale=pscol(l, g)))

                sl_ba, sv_ba = load_slab(wview(w_ba[l], 0, 1024), 4, 1024)
                for half in range(2):
                    sl, sv = load_slab(wview(w_in_l, C_GA + 512 * half, 512), 8, 512)
                    for jj in range(4):
                        j = half * 4 + jj
                        for n, (n0, N) in enumerate(subs):
                            b = proj_group(8, lambda k: sv[:, k, jj * 128:(jj + 1) * 128], lambda k: hT[:, k, n0:n0 + N],
                                           [("wslot", sl)] + [("hT", k, n) for k in range(8)], n0, N)
                            q = nxt("sg", 3)
                            S.op("act", [("ps", b)], [("sg", q)],
                                 lambda e, b=b, q=q: e.activation(out=sgring[q][:, 0:N], in_=ps[b][:, 0:N],
                                                                  func=AF.Sigmoid))
                            b2 = proj_group(4, lambda k: sv_ba[:, k, j * 128:(j + 1) * 128],
                                            lambda k: pT[:, k, n0:n0 + N],
                                            [("wslot", sl_ba)] + [("pT", k, n) for k in range(4)], n0, N)
                            S.op("dve", [("ps", b2), ("sg", q)], [("B2", j, n), "R2b"],
                                 lambda e, b2=b2, q=q, j=j: e.tensor_tensor(out=B2[:, j, n0:n0 + N], in0=ps[b2][:, 0:N],
                                                                            in1=sgring[q][:, 0:N], op=ALU.mult))

                v_finish()

                for half in range(2):
                    sl, sv = load_slab(wview(w_in_l, C_GB + 512 * half, 512), 8, 512)
                    for hh in range(4):
                        h = half * 4 + hh
                        for n, (n0, N) in enumerate(subs):
                            tiles = list(range(n0 // 128, (n0 + N) // 128))
                            b = bank()

                            def emit(e, tiles=tiles, b=b, h=h):
                                last = None
                                for ti, i in enumerate(tiles):
                                    Wt = WsT if (samp and i == NT - 1) else WmT
                                    last = e.matmul(ps[b][:, ti * 128:(ti + 1) * 128], lhsT=B3[:, i, h * 128:(h + 1) * 128],
                                                    rhs=Wt[:, h, :], start=True, stop=True)
                                return last
                            S.op("pe", [("B3", i, hf) for i in tiles for hf in range(2)] + ["R2c", ("WmT", (h // 4) * 4), ("WsT", (h // 4) * 4)],
                                 [("ps", b)], emit)
                            npt = len(tiles) - (1 if (samp and tiles[-1] == NT - 1) else 0)
                            if npt > 0:
                                S.op("dve", [("ps", b), ("Cm", (h // 4) * 4), "pcB"], [("ps", b)],
                                     lambda e, b=b, h=h, npt=npt: e.scalar_tensor_tensor(
                                         out=ps[b][:, 0:npt * 128].rearrange("p (a t) -> p a t", a=npt),
                                         in0=ps[b][:, 0:npt * 128].rearrange("p (a t) -> p a t", a=npt),
                                         scalar=lngcol(l, h),
                                         in1=Cm[:, h, :].unsqueeze(1).broadcast_to([128, npt, 128]),
                                         op0=ALU.mult, op1=ALU.add))
                            if npt < len(tiles):
                                S.op("dve", [("ps", b), ("Cs", (h // 4) * 4), "pcB"], [("ps", b)],
                                     lambda e, b=b, h=h, npt=npt: e.scalar_tensor_tensor(
                                         out=ps[b][:, npt * 128:(npt + 1) * 128],
                                         in0=ps[b][:, npt * 128:(npt + 1) * 128], scalar=lngcol(l, h),
                                         in1=Cs[:, h, :], op0=ALU.mult, op1=ALU.add))
                            S.op("dve", [("ps", b), ("B0", h, n)], [("B0", h, n), "R2a"],
                                 lambda e, b=b, h=h: e.tensor_tensor(out=B0[:, h, n0:n0 + N], in0=B0[:, h, n0:n0 + N],
                                                                     in1=ps[b][:, 0:N], op=ALU.mult))
                        for n, (n0, N) in enumerate(subs):
                            b = proj_group(8, lambda k: sv[:, k, hh * 128:(hh + 1) * 128], lambda k: hT[:, k, n0:n0 + N],
                                           [("wslot", sl)] + [("hT", k, n) for k in range(8)], n0, N)
                            S.op("act", [("ps", b)], [("B1", h, n), "R2d"],
                                 lambda e, b=b, h=h: e.activation(out=B1[:, h, n0:n0 + N], in_=ps[b][:, 0:N],
                                                                  func=AF.Sigmoid))

                for half in range(2):
                    sl2, sv2 = load_slab(wview(w_bb[l], 512 * half, 512), 8, 512)
                    for jj in range(4):
                        j = half * 4 + jj
                        for n, (n0, N) in enumerate(subs):
                            b2 = proj_group(8, lambda k: sv2[:, k, jj * 128:(jj + 1) * 128],
                                            lambda k: B0[:, k, n0:n0 + N],
                                            [("wslot", sl2), "R2a"] + [("B0", k, n) for k in range(8)], n0, N)
                            S.op("dve", [("ps", b2), ("B1", j, n)], [("ps", b2)],
                                 lambda e, b2=b2, j=j: e.tensor_tensor(out=ps[b2][:, 0:N], in0=ps[b2][:, 0:N],
                                                                       in1=B1[:, j, n0:n0 + N], op=ALU.mult))
                            S.op("dve", [("ps", b2), ("B2", j, n)], [("B2", j, n), "R2b"],
                                 lambda e, b2=b2, j=j: e.tensor_tensor(out=B2[:, j, n0:n0 + N], in0=B2[:, j, n0:n0 + N],
                                                                       in1=ps[b2][:, 0:N], op=ALU.add))

                slabs = {}

                def wout_slab(j, slabs=slabs, l=l):
                    half = j // 4
                    if half not in slabs:
                        slabs[half] = load_slab(wview(w_out[l], 512 * half, 512), 8, 512)
                    sl, sv = slabs[half]
                    return sl, sv, (j % 4) * 128
                out_proj_with_stats(8, B2, "B2", ["R2b"], wout_slab, subs, tail=8,
                                    mid_cb=lambda: post_norm_sub(l, 1, 0, subs[0][0], subs[0][1]))
                ffn_h(l, 0, subs[0][0], subs[0][1])
                ffn_boundary_sub(l, 1, subs[1][0], subs[1][1])
                ffn_stats = ffn_stats_fn(l, subs)

                nx = next_layer(bi, l)
                if nx is not None:
                    all_param_dmas(nx[1], BLOCKS[nx[0]][2])

                RK = ["R2a", "R2b", "R2c", "R2d"]
                def up_group(sl, sv, s8, jj, n, n0, N):
                    j = s8 * 4 + jj
                    b = proj_group(8, lambda k: sv[:, k, jj * 128:(jj + 1) * 128], lambda k: hT[:, k, n0:n0 + N],
                                   [("wslot", sl)] + [("hT", k, n) for k in range(8)], n0, N)
                    S.op("act", [("ps", b)], [("fT", j, n), RK[j // 8]],
                         lambda e: e.activation(out=fT[:, j, n0:n0 + N], in_=ps[b][:, 0:N], func=AF.Relu))
                    S.op("dve", [("fT", j, n)], [("fT", j, n), RK[j // 8]],
                         lambda e: e.tensor_tensor(out=fT[:, j, n0:n0 + N], in0=fT[:, j, n0:n0 + N],
                                                   in1=fT[:, j, n0:n0 + N], op=ALU.mult))
                up01 = [load_slab(wview(w_up[l], 512 * s8, 512), 8, 512) for s8 in range(3)]
                def stats_step(n):
                    if ffn_stats[n]:
                        sq_, mm_ = ffn_stats[n].pop(0)
                        if sq_ is not None:
                            sq_()
                        if mm_ is not None:
                            mm_()
                for n, (n0, N) in enumerate(subs):
                    for s8 in range(3):
                        for jj in range(4):
                            up_group(up01[s8][0], up01[s8][1], s8, jj, n, n0, N)
                            stats_step(n)
                    while ffn_stats[n]:
                        stats_step(n)
                for s8 in range(3, 8):
                    sl, sv = load_slab(wview(w_up[l], 512 * s8, 512), 8, 512)
                    for jj in range(4):
                        for n, (n0, N) in enumerate(subs):
                            up_group(sl, sv, s8, jj, n, n0, N)
                    if nx is not None and s8 in (3, 5):
                        prep_compute(nx[1], BLOCKS[nx[0]][2], phases=(0,) if s8 == 3 else (1,))

                if nx is not None:
                    prep_compute(nx[1], BLOCKS[nx[0]][2], phases=(2,))

                dslabs = {}

                def wdown_slab(j, l=l, dslabs=dslabs):
                    if j not in dslabs:
                        dslabs[j] = load_slab(wview(w_down[l], 128 * j, 128), 32, 128)
                    return dslabs[j][0], dslabs[j][1], 0
                has_next = l + 1 < DEPTH

                def mid_down():
                    post_norm_sub(l, 3, 0, subs[0][0], subs[0][1], corr=True)
                    if has_next:
                        pre_norm_sq(0, subs[0][0], subs[0][1])

                def late_down():
                    if has_next:
                        pre_norm_fin(l + 1, 0, 0, subs[0][0], subs[0][1])
                out_proj_with_stats(32, fT, "fT", RK, wdown_slab, subs, tail=3, mid_cb=mid_down, late_cb=late_down)
                post_norm_sub(l, 3, 1, subs[1][0], subs[1][1], corr=True)
                if has_next:
                    pre_norm_sq(1, subs[1][0], subs[1][1])
                    deferred_fin.append(lambda l=l: pre_norm_fin(l + 1, 0, 1, subs[1][0], subs[1][1]))

            for i in range(NT):
                v = nxt("vtm", 2)
                n = tile_sub(i)
                for half in range(2):
                    b = bank()
                    for cc in range(4):
                        c = half * 4 + cc
                        S.op("pe", [("xT", c, n), "ident"], [("ps", b)] if cc == 0 else [("psx", b, cc)],
                             lambda e, b=b, c=c, cc=cc, i=i: e.transpose(out=ps[b][:, cc * 128:(cc + 1) * 128],
                                                                         in_=xT[:, c, i * 128:(i + 1) * 128],
                                                                         identity=ident[:]))
                    copy_op(alt_eng(), [("ps", b), ("psx", b, 1), ("psx", b, 2), ("psx", b, 3)],
                            [("vtm", v)] if half == 0 else [("vtmx", v)],
                            vtm[v][:, half * 512:(half + 1) * 512], ps[b][:, :])
                dst = ys if (samp and i == NT - 1) else yp[p0 + 128 * i: p0 + 128 * (i + 1), :]
                S.dma([("vtm", v), ("vtmx", v)], [("yout", bi, i)], dst, vtm[v][:])
        S.finish()
    return nc


_CONSTS = None


def _consts():
    global _CONSTS
    if _CONSTS is None:
        ident = np.eye(128, dtype=np.float32)
        s = np.arange(128)
        tril = (s[:, None] <= s[None, :]).astype(np.float32)
        bmask = ((s[:, None] // 8) == (s[None, :] // 8)).astype(np.float32)
        rep = np.zeros((128, 128), np.float32)
        rep[s % 8, s] = 1.0
        inv = np.zeros((4, 15), np.float32)
        for g, w in enumerate((2, 4, 8, 16)):
            for pos in range(15):
                inv[g, pos] = 1.0 / min(w, pos + 1)
        invcnt = np.ascontiguousarray(np.broadcast_to(inv.reshape(1, 60), (128, 60)))
        _CONSTS = dict(c_ident=ident, c_tril=tril, c_bmask=bmask, c_rep=rep, c_invcnt=invcnt)
    return _CONSTS


_NC = None


def kernel(x_prompt, x_sample, state_pool, w_in, w_pool_grp, pool_scale, w_spatial, b_spatial,
           ln_v_g, ln_v_b, w_branch_a, w_branch_b, w_out, w_up, w_down,
           g_pre_mix, g_post_mix, g_pre_ffn, g_post_ffn):
    global _NC
    f = lambda a: np.ascontiguousarray(np.asarray(a, dtype=np.float32))
    if _NC is None:
        _NC = build()
    nc = _NC
    x_prompt = f(x_prompt); x_sample = f(x_sample); state_pool = f(state_pool)
    shared = dict(w_in=f(w_in), w_pg=f(w_pool_grp), pool_scale=f(pool_scale), w_sp=f(w_spatial),
                  b_sp=f(b_spatial).reshape(DEPTH, 1024), ln_g=f(ln_v_g), ln_b=f(ln_v_b), w_ba=f(w_branch_a),
                  w_bb=f(w_branch_b), w_out=f(w_out), w_up=f(w_up), w_down=f(w_down), g_pre_mix=f(g_pre_mix),
                  g_post_mix=f(g_post_mix), g_pre_ffn=f(g_pre_ffn), g_post_ffn=f(g_post_ffn))
    shared.update(_consts())
    in_maps = []
    for c in range(NCORES):
        m = dict(shared)
        m["xp"] = x_prompt[c]
        m["xs"] = np.ascontiguousarray(x_sample[16 * c:16 * (c + 1)].reshape(128, D))
        m["sp"] = np.ascontiguousarray(state_pool[:, 16 * c:16 * (c + 1)].reshape(DEPTH, 240, 512))
        in_maps.append(m)
    res = run_bass_kernel_spmd(nc, in_maps, core_ids=list(range(NCORES)))
    r = res.results
    y_prompt = np.stack([r[c]["yp"] for c in range(NCORES)], axis=0)
    y_sample = np.concatenate([r[c]["ys"].reshape(16, 8, D) for c in range(NCORES)], axis=0)
    spp = np.stack([r[c]["spp"] for c in range(NCORES)], axis=1)
    sps = np.concatenate([r[c]["sps"] for c in range(NCORES)], axis=1)
    svs = np.concatenate([r[c]["svs"].reshape(DEPTH, 16, 8, D) for c in range(NCORES)], axis=1)
    return (y_prompt.astype(np.float32), y_sample.astype(np.float32), spp.astype(np.float32),
            sps.astype(np.float32), svs.astype(np.float32))
```

```python
import numpy as np
from contextlib import ExitStack
import concourse.bass as bass
import concourse.mybir as mybir
from concourse.bass_utils import run_bass_kernel_spmd

F32 = mybir.dt.float32
BF16 = mybir.dt.bfloat16
AF = mybir.ActivationFunctionType
ALU = mybir.AluOpType

D = 1024
DIN = 4608
DFF = 4096
DEPTH = 4
NCORES = 8
SEQ = 2048
EPS = 1e-6
TMAX = 768
BLOCKS = [(0, 768, False), (768, 768, False), (1536, 512, True)]
C_A, C_U, C_V, C_GA, C_GB = 0, 512, 1536, 2560, 3584
NSLOT = 4
SLOT_ELEMS = 4096
ROT_BANKS = [0, 1, 2, 3]
DTMP_BANKS = [4, 7]
SS_BANKS = [5, 6]
NRING = 8
SAME_ENGINE_SYNC = True


class Sched:
    def __init__(self, nc, es):
        self.nc = nc
        self.es = es
        self.eng = {"pe": nc.tensor, "act": nc.scalar, "dve": nc.vector, "pool": nc.gpsimd, "sp": nc.sync}
        self.sems = {}
        self.val = {}
        self.seen = {e: {} for e in self.eng}
        self.lw = {}
        self.rd = {}
        for e in ("pe", "act", "dve", "pool"):
            self.add_sem(e)
        self.ring = 0
        for i in range(NRING):
            self.add_sem(("d", i))
        for s in range(NSLOT):
            self.add_sem(("w", s))

    def add_sem(self, key):
        name = "s_" + "_".join(str(k) for k in (key if isinstance(key, tuple) else (key,)))
        self.sems[key] = self.es.enter_context(self.nc.semaphore(name))
        self.val[key] = 0

    def op(self, eng, reads, writes, emit, semkey=None, inc=1, extra=()):
        e = self.eng[eng]
        own = semkey if semkey is not None else eng
        deps = {}

        def add(d):
            if d is None:
                return
            k, v = d
            if k == eng and (eng == "pe" or not SAME_ENGINE_SYNC):
                return
            if deps.get(k, 0) < v:
                deps[k] = v

        for r in reads:
            add(self.lw.get(r))
        for w in writes:
            add(self.lw.get(w))
            for k, v in self.rd.get(w, {}).items():
                add((k, v))
        for d in extra:
            add(d)
        for k, v in deps.items():
            if self.seen[eng].get(k, 0) >= v:
                continue
            e.wait_ge(self.sems[k], v)
            self.seen[eng][k] = v
        inst = emit(e)
        self.val[own] += inc
        inst.then_inc(self.sems[own], inc)
        tick = (own, self.val[own])
        for w in writes:
            self.lw[w] = tick
            self.rd[w] = {}
        for r in reads:
            d = self.rd.setdefault(r, {})
            if d.get(own, 0) < tick[1]:
                d[own] = tick[1]
        return tick

    def dma(self, reads, writes, out, in_):
        i = self.ring
        self.ring = (self.ring + 1) % NRING
        key = ("d", i)
        prev = (key, self.val[key])
        return self.op("sp", reads, writes, lambda e: e.dma_start(out=out, in_=in_), semkey=key, inc=16,
                       extra=(prev,) if prev[1] > 0 else ())

    def finish(self):
        e = self.eng["sp"]
        for i in range(NRING):
            key = ("d", i)
            if self.val[key] > 0:
                e.wait_ge(self.sems[key], self.val[key])


def build():
    nc = bass.Bass("TRN2", target_bir_lowering=False)

    def din(name, shape):
        return nc.dram_tensor(name, shape, F32, kind="ExternalInput").ap()

    def dout(name, shape):
        return nc.dram_tensor(name, shape, F32, kind="ExternalOutput").ap()

    xp = din("xp", [SEQ, D])
    xs = din("xs", [128, D])
    sp_in = din("sp", [DEPTH, 240, 512])
    w_in = din("w_in", [DEPTH, D, DIN])
    w_pg = din("w_pg", [DEPTH, 4, 128, 128])
    pool_scale = din("pool_scale", [DEPTH, 512])
    w_sp = din("w_sp", [DEPTH, 8, 128, 128])
    b_sp = din("b_sp", [DEPTH, 1024])
    ln_g = din("ln_g", [DEPTH, 1024])
    ln_b = din("ln_b", [DEPTH, 1024])
    w_ba = din("w_ba", [DEPTH, 512, D])
    w_bb = din("w_bb", [DEPTH, D, D])
    w_out = din("w_out", [DEPTH, D, D])
    w_up = din("w_up", [DEPTH, D, DFF])
    w_down = din("w_down", [DEPTH, DFF, D])
    g_all = [din(n, [DEPTH, D]) for n in ("g_pre_mix", "g_post_mix", "g_pre_ffn", "g_post_ffn")]
    c_ident = din("c_ident", [128, 128])
    c_tril = din("c_tril", [128, 128])
    c_bmask = din("c_bmask", [128, 128])
    c_rep = din("c_rep", [128, 128])
    c_invcnt = din("c_invcnt", [128, 60])

    yp = dout("yp", [SEQ, D])
    ys = dout("ys", [128, D])
    spp = dout("spp", [DEPTH, 15, 512])
    sps = dout("sps", [DEPTH, 16, 15, 512])
    svs = dout("svs", [DEPTH, 128, D])

    es = ExitStack()
    with es:
        S = Sched(nc, es)

        def sb(name, shape, dt):
            return es.enter_context(nc.sbuf_tensor(name, shape, dt))

        xT = sb("xT", [128, 8, TMAX], F32)
        hT = sb("hT", [128, 8, TMAX], BF16)
        R1 = sb("R1", [128, 6784], F32)
        R2 = sb("R2", [128, 32 * TMAX], BF16)
        AW = 896
        aext = R1[:, 0:4 * AW].rearrange("p (g w) -> p g w", g=4)
        dT = R1[:, 3584:3584 + 1536].bitcast(BF16).rearrange("p (g t) -> p g t", g=4)
        pT = R1[:, 5120:5120 + 1536].bitcast(BF16).rearrange("p (g t) -> p g t", g=4)
        oT = R1[:, 0:8 * TMAX].rearrange("p (c t) -> p c t", c=8)
        B0 = R2[:, 0:8 * TMAX].rearrange("p (c t) -> p c t", c=8)
        B2 = R2[:, 8 * TMAX:16 * TMAX].rearrange("p (c t) -> p c t", c=8)
        B3 = R2[:, 16 * TMAX:24 * TMAX].rearrange("p (i f) -> p i f", f=1024)
        B1 = R2[:, 24 * TMAX:32 * TMAX].rearrange("p (c t) -> p c t", c=8)
        fT = R2[:, :].rearrange("p (c t) -> p c t", c=32)
        sgring = [sb(f"sg{i}", [128, 384], BF16) for i in range(3)]
        sqring = [sb(f"sq{i}", [128, 384], BF16) for i in range(4)]
        tmpring = [sb(f"tmp{i}", [128, 512], F32) for i in range(3)]
        vtm = [sb(f"vtm{i}", [128, 1024], F32) for i in range(2)]
        wslot = [sb(f"wslot{i}", [128, SLOT_ELEMS], BF16) for i in range(NSLOT)]
        pcA = sb("pcA", [128, 128], F32)
        pcB = sb("pcB", [128, 48], F32)
        wpg = sb("wpg", [128, 4, 128], BF16)
        wsp = sb("wsp", [128, 8, 128], F32)
        WmT = sb("WmT", [128, 8, 128], BF16)
        WsT = sb("WsT", [128, 8, 128], BF16)
        Cm = sb("Cm", [128, 8, 128], F32)
        Cs = sb("Cs", [128, 8, 128], F32)
        bb16 = sb("bb16", [128, 1024], BF16)
        bsb = sb("bsb", [128, 8, 128], F32)
        gb32 = sb("gb32", [128, 1024], F32)
        bb32 = sb("bb32", [128, 1024], F32)
        ident = sb("ident", [128, 128], F32)
        ones16 = sb("ones16", [128, 128], BF16)
        trilT = sb("trilT", [128, 128], F32)
        bmask = sb("bmask", [128, 128], F32)
        rep32 = sb("rep32", [128, 128], F32)
        rep16 = sb("rep16", [128, 128], BF16)
        invcnt = sb("invcnt", [128, 4, 15], F32)
        atail = sb("atail", [128, DEPTH, 4, 15], F32)
        anew = sb("anew", [128, 4, 128], F32)
        t3 = sb("t3", [128, 16], F32)
        t3p = sb("t3p", [128, 16], F32)
        ptmp = [sb(f"ptmp{i}", [128, 400], F32) for i in range(2)]
        stats = [sb(f"stats{i}", [128, 2, 6], F32) for i in range(2)]
        mv = [sb(f"mv{i}", [128, 2], F32) for i in range(2)]
        sm = [sb(f"sm{i}", [128, 4], F32) for i in range(2)]
        pstage = sb("pstage", [128, 176], F32)
        vstats = sb("vstats", [128, 6, 2, 6], F32)
        vmv = sb("vmv", [128, 6, 2], F32)
        vrs = sb("vrs", [128, 8], F32)
        vnm = sb("vnm", [128, 8], F32)
        ps = [es.enter_context(nc.psum_tensor(f"ps{i}", [128, 512], F32)) for i in range(8)]

        st = {"rot": 0, "sq": 0, "sg": 0, "tmp": 0, "vtm": 0, "slot": 0, "small": 0, "alt": 0, "dtmp": 0}

        def nxt(name, n):
            v = st[name]
            st[name] = (v + 1) % n
            return v

        def bank():
            return ROT_BANKS[nxt("rot", len(ROT_BANKS))]

        def alt_eng():
            st["alt"] ^= 1
            return "act" if st["alt"] else "dve"

        def copy_op(eng, reads, writes, out, in_):
            if eng == "act":
                return S.op("act", reads, writes, lambda e: e.activation(out=out, in_=in_, func=AF.Copy))
            return S.op("dve", reads, writes, lambda e: e.tensor_copy(out=out, in_=in_))

        S.dma([], ["ident"], ident[:], c_ident)
        S.dma([], ["trilT"], trilT[:], c_tril)
        S.dma([], ["bmask"], bmask[:], c_bmask)
        S.dma([], ["rep32"], rep32[:], c_rep)
        S.dma([], ["invcnt"], invcnt[:].rearrange("p g w -> p (g w)"), c_invcnt)
        S.op("dve", [], ["ones16"], lambda e: e.memset(ones16[:], 1.0 / 1024.0))
        S.op("dve", ["rep32"], ["rep16"], lambda e: e.tensor_copy(out=rep16[:], in_=rep32[:]))
        for k in range(4):
            S.dma([], [("pstage", k)], pstage[32 * k:32 * (k + 1), 0:128], g_all[k].rearrange("l (c p) -> (l c) p", p=128))
        pstB = sb("pstB", [128, 128], F32)
        S.dma([], ["pstB"], pstB[0:32, :], ln_g.rearrange("l (c p) -> (l c) p", p=128))
        S.dma([], ["pstB2"], pstB[32:48, :], pool_scale.rearrange("l (c p) -> (l c) p", p=128))
        b = bank()
        S.op("pe", [("pstage", k) for k in range(4)] + ["ident"], [("ps", b)],
             lambda e: e.transpose(out=ps[b][:, 0:128], in_=pstage[:, 0:128], identity=ident[:]))
        S.op("dve", [("ps", b)], ["pcA"], lambda e: e.tensor_copy(out=pcA[:], in_=ps[b][:, 0:128]))
        b = bank()
        S.op("pe", ["pstB", "pstB2", "ident"], [("ps", b)],
             lambda e: e.transpose(out=ps[b][:, 0:48], in_=pstB[0:48, :], identity=ident[0:48, 0:48]))
        S.op("dve", [("ps", b)], ["pcB"], lambda e: e.tensor_copy(out=pcB[:], in_=ps[b][:, 0:48]))

        def gcol(kind, l, c):
            return pcA[:, kind * 32 + l * 8 + c: kind * 32 + l * 8 + c + 1]

        def lngcol(l, c):
            return pcB[:, l * 8 + c: l * 8 + c + 1]

        def pscol(l, g):
            return pcB[:, 32 + l * 4 + g: 32 + l * 4 + g + 1]

        def load_slab(src_ap, kc, ncols):
            s = nxt("slot", NSLOT)
            dst = wslot[s][:, 0:kc * ncols].rearrange("p (k n) -> p k n", k=kc)
            S.op("pool", [], [("wslot", s)], lambda e: e.dma_start(out=dst, in_=src_ap), semkey=("w", s), inc=16)
            return s, dst

        def wview(w2d, c0, ncols):
            return w2d.rearrange("(k p) n -> p k n", p=128)[:, :, c0:c0 + ncols]

        def param_dmas(l, samp):
            S.dma([], ["wsp"], wsp[:], w_sp[l].rearrange("h t s -> t h s"))
            S.dma([], ["bsb"], bsb[:].rearrange("p h t -> p (h t)"), b_sp[l:l + 1, :].broadcast_to([128, 1024]))
            if samp:
                S.dma([], ["gb32"], gb32[:], ln_g[l:l + 1, :].broadcast_to([128, 1024]))
                S.dma([], ["bb32"], bb32[:], ln_b[l:l + 1, :].broadcast_to([128, 1024]))

        wpg32 = sb("wpg32", [128, 4, 128], F32)

        def param_dmas2(l):
            S.dma([], ["wpg32"], wpg32[:], w_pg[l].rearrange("g c d -> c g d"))

        def layer_prep(l, samp, phases=(0, 1, 2)):
            if 0 in phases:
                S.op("dve", ["wpg32"], ["wpg"], lambda e: e.tensor_copy(out=wpg[:], in_=wpg32[:]))
            for h0 in ((0, 4) if 0 in phases else ()):
                b = bank()
                for hh in range(4):
                    h = h0 + hh
                    S.op("pe", ["wsp", "ident"], [("ps", b)] if hh == 0 else [("psx", b, hh)],
                         lambda e, h=h, hh=hh, b=b: e.transpose(out=ps[b][:, hh * 128:(hh + 1) * 128], in_=wsp[:, h, :],
                                                                identity=ident[:]))
                S.op("dve", [("ps", b), ("psx", b, 1), ("psx", b, 2), ("psx", b, 3), "trilT"], [("WmT", h0)],
                     lambda e, b=b, h0=h0: e.tensor_tensor(
                         out=WmT[:, h0:h0 + 4, :], in0=ps[b][:, :].rearrange("p (a t) -> p a t", a=4),
                         in1=trilT[:].unsqueeze(1).broadcast_to([128, 4, 128]), op=ALU.mult))
            if samp and 1 in phases:
                for h0 in (0, 4):
                    b = bank()
                    for hh in range(4):
                        h = h0 + hh
                        S.op("pe", [("WmT", h0), "rep16"], [("ps", b)] if hh == 0 else [("psx", b, hh)],
                             lambda e, h=h, hh=hh, b=b: e.matmul(
                                 ps[b][:, hh * 128:(hh + 1) * 128], lhsT=rep16[:, :],
                                 rhs=WmT[:, h, 0:8].unsqueeze(1).broadcast_to([128, 16, 8]), start=True, stop=True))
                    S.op("dve", [("ps", b), ("psx", b, 1), ("psx", b, 2), ("psx", b, 3), "bmask"], [("WsT", h0)],
                         lambda e, b=b, h0=h0: e.tensor_tensor(
                             out=WsT[:, h0:h0 + 4, :], in0=ps[b][:, :].rearrange("p (a t) -> p a t", a=4),
                             in1=bmask[:].unsqueeze(1).broadcast_to([128, 4, 128]), op=ALU.mult))
            for (Wt, Ct, key, ckey) in (([(WmT, Cm, "WmT", "Cm")] + ([(WsT, Cs, "WsT", "Cs")] if samp else []))
                                        if 2 in phases else []):
                for h0 in (0, 4):
                    b = bank()
                    for hh in range(4):
                        h = h0 + hh
                        S.op("pe", [(key, h0), "bb16"], [("ps", b)] if hh == 0 else [("psx", b, hh)],
                             lambda e, h=h, hh=hh, b=b, Wt=Wt: e.matmul(
                                 ps[b][:, hh * 128:(hh + 1) * 128], lhsT=bb16[:, h * 128:(h + 1) * 128],
                                 rhs=Wt[:, h, :], start=True, stop=True))
                    if ckey == "Cm":
                        S.op("dve", [("ps", b), ("psx", b, 1), ("psx", b, 2), ("psx", b, 3), "bsb"], [(ckey, h0)],
                             lambda e, b=b, h0=h0, Ct=Ct: e.tensor_tensor(
                                 out=Ct[:, h0:h0 + 4, :], in0=ps[b][:, :].rearrange("p (a t) -> p a t", a=4),
                                 in1=bsb[:, h0:h0 + 4, :], op=ALU.add))
                    else:
                        for hh in range(4):
                            S.op("dve", [("ps", b), ("psx", b, 1), ("psx", b, 2), ("psx", b, 3), "bsb"], [(ckey, h0)],
                                 lambda e, b=b, h0=h0, hh=hh, Ct=Ct: e.tensor_tensor(
                                     out=Ct[:, h0 + hh, :].rearrange("p (q r) -> p q r", r=8),
                                     in0=ps[b][:, hh * 128:(hh + 1) * 128].rearrange("p (q r) -> p q r", r=8),
                                     in1=bsb[:, h0 + hh, 0:8].unsqueeze(1).broadcast_to([128, 16, 8]), op=ALU.add))

        def norm_stats_finish(n, N, corr=False):
            bk = SS_BANKS[n]
            if corr:
                S.op("dve", [("ss", n), ("ptmp", n)], [("ss", n)],
                     lambda e: e.tensor_tensor(out=ps[bk][:, 0:N], in0=ps[bk][:, 0:N], in1=ptmp[n][:, 0:N], op=ALU.add))
            S.op("act", [("ss", n)], [("ss", n)],
                 lambda e: e.activation(out=ps[bk][:, 0:N], in_=ps[bk][:, 0:N], func=AF.Ln,
                                        bias=(0.0 if corr else EPS), scale=1.0))
            S.op("act", [("ss", n)], [("ss", n)],
                 lambda e: e.activation(out=ps[bk][:, 0:N], in_=ps[bk][:, 0:N], func=AF.Exp, scale=-0.5))

        def pre_norm_sub(l, kind, n, n0, N):
            bk = SS_BANKS[n]
            for c in range(8):
                q = nxt("sq", 4)
                S.op("act", [("xT", c, n)], [("sq", q)],
                     lambda e, c=c, q=q: e.activation(out=sqring[q][:, 0:N], in_=xT[:, c, n0:n0 + N], func=AF.Square))
                S.op("pe", [("sq", q), "ones16"], [("ss", n)],
                     lambda e, c=c, q=q: e.matmul(ps[bk][:, 0:N], lhsT=ones16[:], rhs=sqring[q][:, 0:N],
                                                  start=(c == 0), stop=(c == 7)))
            norm_stats_finish(n, N)
            for c in range(8):
                S.op("dve", [("xT", c, n), ("ss", n), "pcA"], [("hT", c, n)],
                     lambda e, c=c: e.scalar_tensor_tensor(out=hT[:, c, n0:n0 + N], in0=xT[:, c, n0:n0 + N],
                                                           scalar=gcol(kind, l, c), in1=ps[bk][:, 0:N],
                                                           op0=ALU.mult, op1=ALU.mult))

        def pre_norm(l, kind, subs):
            for n, (n0, N) in enumerate(subs):
                pre_norm_sub(l, kind, n, n0, N)

        def pre_norm_sq(n, n0, N):
            for c in range(8):
                S.op("act", [("xT", c, n)], [("hT", c, n)],
                     lambda e, c=c: e.activation(out=hT[:, c, n0:n0 + N], in_=xT[:, c, n0:n0 + N], func=AF.Square))

        def pre_norm_fin(l, kind, n, n0, N):
            bk = SS_BANKS[n]
            for c in range(8):
                S.op("pe", [("hT", c, n), "ones16"], [("ss", n)],
                     lambda e, c=c: e.matmul(ps[bk][:, 0:N], lhsT=ones16[:], rhs=hT[:, c, n0:n0 + N],
                                             start=(c == 0), stop=(c == 7)))
            norm_stats_finish(n, N)
            for c in range(8):
                S.op("dve", [("xT", c, n), ("ss", n), "pcA"], [("hT", c, n)],
                     lambda e, c=c: e.scalar_tensor_tensor(out=hT[:, c, n0:n0 + N], in0=xT[:, c, n0:n0 + N],
                                                           scalar=gcol(kind, l, c), in1=ps[bk][:, 0:N],
                                                           op0=ALU.mult, op1=ALU.mult))

        def proj_group(kc, lhs_fn, rhs_fn, reads, n0, N):
            b = bank()

            def emit(e):
                last = None
                for k in range(kc):
                    last = e.matmul(ps[b][:, 0:N], lhsT=lhs_fn(k), rhs=rhs_fn(k), start=(k == 0), stop=(k == kc - 1))
                return last
            S.op("pe", reads, [("ps", b)], emit)
            return b

        def post_norm_sub(l, kind, n, n0, N, corr=False):
            bk = SS_BANKS[n]
            norm_stats_finish(n, N, corr)
            pend = []

            def do_add(c, t):
                S.op("dve", [("ps", t), ("xT", c, n)], [("xT", c, n)],
                     lambda e: e.tensor_tensor(out=xT[:, c, n0:n0 + N], in0=xT[:, c, n0:n0 + N],
                                               in1=ps[t][:, 0:N], op=ALU.add))
            for c in range(8):
                t = DTMP_BANKS[nxt("dtmp", 2)]
                S.op("dve", [("oT", c, n), ("R1", n), ("ss", n), "pcA"], [("ps", t)],
                     lambda e, c=c, t=t: e.scalar_tensor_tensor(out=ps[t][:, 0:N], in0=oT[:, c, n0:n0 + N],
                                                                scalar=gcol(kind, l, c), in1=ps[bk][:, 0:N],
                                                                op0=ALU.mult, op1=ALU.mult))
                pend.append((c, t))
                if len(pend) > 1:
                    do_add(*pend.pop(0))
            while pend:
                do_add(*pend.pop(0))

        def norm_boundary(l_post, kind_post, pre, subs, corr=False):
            for n, (n0, N) in enumerate(subs):
                post_norm_sub(l_post, kind_post, n, n0, N, corr)
                if pre is not None:
                    pre_norm_sub(pre[0], pre[1], n, n0, N)

        def ffn_boundary_sub(l, n, n0, N):
            post_norm_sub(l, 1, n, n0, N)
            ffn_h(l, n, n0, N)

        def ffn_h(l, n, n0, N):
            for c in range(8):
                S.op("act", [("xT", c, n), "pcA"], [("hT", c, n)],
                     lambda e, c=c: e.activation(out=hT[:, c, n0:n0 + N], in_=xT[:, c, n0:n0 + N], func=AF.Copy,
                                                 scale=gcol(2, l, c)))

        def ffn_stats_fn(l, subs):
            all_steps = []
            for n, (n0, N) in enumerate(subs):
                bk = SS_BANKS[n]
                slots = {}

                def mk_sq(c, n=n, n0=n0, N=N, slots=slots):
                    def f():
                        q = nxt("sq", 4)
                        slots[c] = q
                        S.op("act", [("xT", c, n)], [("sq", q)],
                             lambda e: e.activation(out=sqring[q][:, 0:N], in_=xT[:, c, n0:n0 + N], func=AF.Square))
                    return f

                def mk_mm(c, n=n, N=N, bk=bk, slots=slots):
                    def f():
                        q = slots[c]
                        S.op("pe", [("sq", q), "ones16"], [("ss", n)],
                             lambda e: e.matmul(ps[bk][:, 0:N], lhsT=ones16[:], rhs=sqring[q][:, 0:N],
                                                start=(c == 0), stop=(c == 7)))
                        if c == 7:
                            S.op("act", [("ss", n)], [("ptmp", n)],
                                 lambda e: e.activation(out=ptmp[n][:, 0:N], in_=ps[bk][:, 0:N], func=AF.Square,
                                                        scale=EPS ** 0.5, bias=EPS ** 1.5))
                    return f
                steps = []
                prev = None
                for c in range(8):
                    steps.append((mk_sq(c), prev))
                    prev = mk_mm(c)
                steps.append((None, prev))
                all_steps.append(steps)
            return all_steps

        def out_proj_with_stats(kc, in_buf, in_key, coarse, slabs_fn, subs, tail=0, mid_cb=None, late_cb=None):
            pend = []

            def do_ss(q, j, n, N):
                bk = SS_BANKS[n]
                S.op("pe", [("sq", q), "ones16"], [("ss", n)],
                     lambda e: e.matmul(ps[bk][:, 0:N], lhsT=ones16[:], rhs=sqring[q][:, 0:N],
                                        start=(j == 0), stop=(j == 7)))

            def group(j, n, n0, N, depth):
                sl, sv, cj = slabs_fn(j)
                b = proj_group(kc, lambda k: sv[:, k, cj:cj + 128], lambda k: in_buf[:, k, n0:n0 + N],
                               [("wslot", sl)] + list(coarse) + [(in_key, k, n) for k in range(kc)], n0, N)
                S.op("act", [("ps", b)], [("oT", j, n), ("R1", n)],
                     lambda e: e.activation(out=oT[:, j, n0:n0 + N], in_=ps[b][:, 0:N], func=AF.Copy))
                q = nxt("sq", 4)
                S.op("act", [("ps", b)], [("sq", q)],
                     lambda e: e.activation(out=sqring[q][:, 0:N], in_=ps[b][:, 0:N], func=AF.Square))
                pend.append((q, j, n, N))
                while len(pend) > depth:
                    do_ss(*pend.pop(0))
            head = 8 - tail
            for j in range(head):
                for n, (n0, N) in enumerate(subs):
                    group(j, n, n0, N, 2)
            if tail == 0:
                while pend:
                    do_ss(*pend.pop(0))
                if mid_cb is not None:
                    mid_cb()
                return
            for n, (n0, N) in enumerate(subs):
                for j in range(head, 8):
                    if n == 1 and j == 7 and late_cb is not None:
                        late_cb()
                    group(j, n, n0, N, 2)
                while pend:
                    do_ss(*pend.pop(0))
                if n == 0 and mid_cb is not None:
                    mid_cb()

        first = True
        for bi, (p0, npr, samp) in enumerate(BLOCKS):
            T = npr + (128 if samp else 0)
            NT = T // 128
            subs = [(0, 384), (384, T - 384)]
            SBASE = 15 + npr

            def tile_sub(i):
                return 0 if i < 3 else 1

            for i in range(NT):
                v = nxt("vtm", 2)
                src = xs if (samp and i == NT - 1) else xp[p0 + 128 * i: p0 + 128 * (i + 1), :]
                S.dma([], [("vtm", v)], vtm[v][:], src)
                n = tile_sub(i)
                for half in range(2):
                    b = bank()
                    for cc in range(4):
                        c = half * 4 + cc
                        S.op("pe", [("vtm", v), "ident"], [("ps", b)] if cc == 0 else [("psx", b, cc)],
                             lambda e, b=b, c=c, cc=cc, v=v: e.transpose(out=ps[b][:, cc * 128:(cc + 1) * 128],
                                                                         in_=vtm[v][:, c * 128:(c + 1) * 128],
                                                                         identity=ident[:]))
                    copy_op(alt_eng(), [("ps", b), ("psx", b, 1), ("psx", b, 2), ("psx", b, 3)],
                            [("xT", half * 4 + cc, n) for cc in range(4)],
                            xT[:, half * 4:half * 4 + 4, i * 128:(i + 1) * 128],
                            ps[b][:, :].rearrange("p (a t) -> p a t", a=4))

            def all_param_dmas(l2, samp2):
                param_dmas(l2, samp2)
                param_dmas2(l2)
                if not samp2:
                    S.dma([], ["bb32"], bb32[:], ln_b[l2:l2 + 1, :].broadcast_to([128, 1024]))

            def prep_compute(l2, samp2, phases=(0, 1, 2)):
                if 0 in phases:
                    S.op("dve", ["bb32"], ["bb16"], lambda e: e.tensor_copy(out=bb16[:], in_=bb32[:]))
                layer_prep(l2, samp2, phases)

            def next_layer(bi2, l2):
                if l2 + 1 < DEPTH:
                    return bi2, l2 + 1
                if bi2 + 1 < len(BLOCKS):
                    return bi2 + 1, 0
                return None

            deferred_fin = []
            for l in range(DEPTH):
                w_in_l = w_in[l]
                if bi == 0 and l == 0:
                    all_param_dmas(0, samp)
                    prep_compute(0, samp)
                if l == 0:
                    pre_norm(l, 0, subs)

                if bi == 0:
                    S.op("dve", [], [("aext", g) for g in range(4)] + [("R1", 0), ("R1", 1)],
                         lambda e: e.memset(aext[:, :, 0:15], 0.0))
                else:
                    S.op("dve", [("atail", l)], [("aext", g) for g in range(4)] + [("R1", 0), ("R1", 1)],
                         lambda e, l=l: e.tensor_copy(out=aext[:, :, 0:15], in_=atail[:, l, :, :]))
                if samp:
                    v = nxt("vtm", 2)
                    stg = vtm[v][:, :].rearrange("p (a f) -> p a f", a=2)
                    for hh in range(2):
                        S.dma([], [("vtm", v)] if hh == 0 else [("vtmx", v)], stg[0:120, hh, :],
                              sp_in[l, hh * 120:(hh + 1) * 120, :])
                    for hh in range(2):
                        b = bank()
                        for g in range(4):
                            S.op("pe", [("vtm", v), ("vtmx", v), "ident"], [("ps", b)] if g == 0 else [("psx", b, g)],
                                 lambda e, b=b, g=g, hh=hh, stg=stg: e.transpose(
                                     out=ps[b][:, g * 128:g * 128 + 120], in_=stg[0:120, hh, g * 128:(g + 1) * 128],
                                     identity=ident[0:120, 0:120]))
                        for g in range(4):
                            dst = aext[:, g, SBASE + hh * 8 * 23: SBASE + (hh + 1) * 8 * 23].rearrange(
                                "p (q r) -> p q r", r=23)[:, :, 0:15]
                            srcp = ps[b][:, g * 128:g * 128 + 120].rearrange("p (q r) -> p q r", r=15)
                            copy_op(alt_eng(), [("ps", b), ("psx", b, 1), ("psx", b, 2), ("psx", b, 3)],
                                    [("aext", g), ("R1", 0), ("R1", 1)], dst, srcp)
                    S.dma([], [("sps_old", l)], sps[l, :, 0:7, :],
                          sp_in[l].rearrange("(q r) f -> q r f", r=15)[:, 8:15, :])

                sl_a, sv_a = load_slab(wview(w_in_l, C_A, 512), 8, 512)
                slu = [load_slab(wview(w_in_l, C_U + 512 * half, 512), 8, 512) for half in range(2)]

                def a_stage(n, n0, N):
                    sl, sv = sl_a, sv_a
                    for j in range(4):
                        b = proj_group(8, lambda k: sv[:, k, j * 128:(j + 1) * 128], lambda k: hT[:, k, n0:n0 + N],
                                       [("wslot", sl)] + [("hT", k, n) for k in range(8)], n0, N)
                        pe_ = min(n0 + N, npr)
                        if pe_ > n0:
                            S.op("act", [("ps", b)], [("aext", j), ("R1", 0), ("R1", 1)],
                                 lambda e, b=b, j=j, pe_=pe_: e.activation(out=aext[:, j, 15 + n0:15 + pe_],
                                                                          in_=ps[b][:, 0:pe_ - n0], func=AF.Copy))
                        if samp and n0 + N > npr:
                            o0 = npr - n0
                            dst = aext[:, j, SBASE:SBASE + 368].rearrange("p (q r) -> p q r", r=23)[:, :, 15:23]
                            S.op("act", [("ps", b)], [("aext", j), ("R1", 0), ("R1", 1)],
                                 lambda e, b=b, dst=dst, o0=o0: e.activation(
                                     out=dst, in_=ps[b][:, o0:o0 + 128].rearrange("p (q r) -> p q r", r=8), func=AF.Copy))
                            S.op("act", [("ps", b)], [("anew", j)],
                                 lambda e, b=b, j=j, o0=o0: e.activation(out=anew[:, j, :], in_=ps[b][:, o0:o0 + 128],
                                                                         func=AF.Copy))

                def u_stage(n, n0, N):
                    for half in range(2):
                        sl, sv = slu[half]
                        for hh in range(4):
                            h = half * 4 + hh
                            b = proj_group(8, lambda k: sv[:, k, hh * 128:(hh + 1) * 128], lambda k: hT[:, k, n0:n0 + N],
                                           [("wslot", sl)] + [("hT", k, n) for k in range(8)], n0, N)
                            S.op("act", [("ps", b)], [("B0", h, n), "R2a"],
                                 lambda e, b=b, h=h: e.activation(out=B0[:, h, n0:n0 + N], in_=ps[b][:, 0:N],
                                                                  func=AF.Gelu_apprx_tanh))

                def a_outputs():
                    if bi < len(BLOCKS) - 1:
                        S.op("act", [("aext", g) for g in range(4)], [("atail", l)],
                             lambda e, l=l: e.activation(out=atail[:, l, :, :], in_=aext[:, :, npr:npr + 15], func=AF.Copy))
                    if samp:
                        b = bank()
                        for g in range(4):
                            S.op("pe", [("aext", g), "ident"], [("ps", b)] if g == 0 else [("psx", b, g)],
                                 lambda e, b=b, g=g: e.transpose(out=ps[b][0:15, g * 128:(g + 1) * 128],
                                                                 in_=aext[:, g, npr:npr + 15], identity=ident[:]))
                        v = nxt("vtm", 2)
                        copy_op("dve", [("ps", b), ("psx", b, 1), ("psx", b, 2), ("psx", b, 3)], [("vtm", v)],
                                vtm[v][0:15, 0:512], ps[b][0:15, :])
                        S.dma([("vtm", v)], [("spp", l)], spp[l], vtm[v][0:15, 0:512])
                        b = bank()
                        for g in range(4):
                            S.op("pe", [("anew", g), "ident"], [("ps", b)] if g == 0 else [("psx", b, g)],
                                 lambda e, b=b, g=g: e.transpose(out=ps[b][:, g * 128:(g + 1) * 128], in_=anew[:, g, :],
                                                                 identity=ident[:]))
                        copy_op("dve", [("ps", b), ("psx", b, 1), ("psx", b, 2), ("psx", b, 3)], [("vtmx", v)],
                                vtm[v][:, 512:1024], ps[b][:, :])
                        for q in range(16):
                            S.dma([("vtmx", v)], [("sps_new", l, q)], sps[l, q, 7:15, :], vtm[v][q * 8:(q + 1) * 8, 512:1024])

                def pool_chain(g, seg, ta, tb, tk, t3x, t3k):
                    kind, c0, L, n = seg
                    w = 2 << g
                    ops = []
                    if kind == "p":
                        def E(off, length):
                            return aext[:, g, 15 + c0 + off: 15 + c0 + off + length]

                        def Tm(t, off, length):
                            return t[:, 15 + off: 15 + off + length]
                        outv = dT[:, g, c0:c0 + L]
                    else:
                        def E(off, length):
                            return aext[:, g, SBASE:SBASE + 368].rearrange("p (q r) -> p q r", r=23)[
                                :, :, 15 + off:15 + off + length]

                        def Tm(t, off, length):
                            return t[:, 0:368].rearrange("p (q r) -> p q r", r=23)[:, :, 15 + off:15 + off + length]
                        outv = dT[:, g, npr:npr + 128].rearrange("p (q r) -> p q r", r=8)
                    keys_r = [("aext", g), ("R1", 0), ("R1", 1)]
                    lo = -14
                    ops.append(lambda lo=lo: S.op("dve", keys_r, [tk[0]],
                               lambda e: e.tensor_tensor(out=Tm(ta, lo, L - lo), in0=E(lo, L - lo), in1=E(lo - 1, L - lo),
                                                         op=ALU.add)))
                    cur, oth, ci = ta, tb, 0
                    sh = 2
                    while sh < w:
                        lo2 = lo + sh
                        ops.append(lambda cur=cur, oth=oth, ci=ci, lo2=lo2, sh=sh: S.op(
                            "dve", [tk[ci]], [tk[1 - ci]],
                            lambda e: e.tensor_tensor(out=Tm(oth, lo2, L - lo2), in0=Tm(cur, lo2, L - lo2),
                                                      in1=Tm(cur, lo2 - sh, L - lo2), op=ALU.add)))
                        lo = lo2
                        cur, oth = oth, cur
                        ci = 1 - ci
                        sh *= 2
                    fix = (bi == 0 and kind == "p" and c0 == 0)
                    if fix:
                        ops.append(lambda cur=cur, ci=ci: S.op(
                            "dve", [tk[ci], "invcnt"], [t3k],
                            lambda e: e.tensor_tensor(out=t3x[:, 0:15], in0=Tm(cur, 0, 15), in1=invcnt[:, g, :],
                                                      op=ALU.mult)))
                    ops.append(lambda cur=cur, ci=ci: S.op(
                        "dve", [tk[ci]] + keys_r, [("dT", g, n), ("R1", 0), ("R1", 1)],
                        lambda e: e.scalar_tensor_tensor(out=outv, in0=Tm(cur, 0, L), scalar=1.0 / w, in1=E(0, L),
                                                         op0=ALU.mult, op1=ALU.subtract)))
                    if fix:
                        ops.append(lambda: S.op(
                            "dve", [t3k] + keys_r, [("dT", g, n), ("R1", 0), ("R1", 1)],
                            lambda e: e.tensor_tensor(out=dT[:, g, 0:15], in0=t3x[:, 0:15], in1=E(0, 15),
                                                      op=ALU.subtract)))
                    return ops

                def pooling(seg):
                    for (gA, gB) in ((3, 2), (1, 0)):
                        A = pool_chain(gA, seg, tmpring[0], tmpring[1], [("tmp", 0), ("tmp", 1)], t3, "t3")
                        Bc = pool_chain(gB, seg, ptmp[0], ptmp[1], [("ptmp", 0), ("ptmp", 1)], t3p, "t3p")
                        for i in range(max(len(A), len(Bc))):
                            if i < len(A):
                                A[i]()
                            if i < len(Bc):
                                Bc[i]()

                for n, (n0, N) in enumerate(subs):
                    if n == 1:
                        while deferred_fin:
                            deferred_fin.pop(0)()
                    a_stage(n, n0, N)
                    if n == len(subs) - 1:
                        a_outputs()
                    pe_ = min(n0 + N, npr)
                    if pe_ > n0:
                        pooling(("p", n0, pe_ - n0, n))
                    if samp and n == len(subs) - 1:
                        pooling(("s", 0, 8, 1))
                    u_stage(n, n0, N)

                slv = [load_slab(wview(w_in_l, C_V + 512 * half, 512), 8, 512) for half in range(2)]
                vsamp = None
                for i in range(NT):
                    n = tile_sub(i)
                    issamp = samp and i == NT - 1
                    if issamp:
                        vsamp = nxt("vtm", 2)
                    for half in range(2):
                        slh, svh = slv[half]
                        b = proj_group(8, lambda k: hT[:, k, i * 128:(i + 1) * 128], lambda k: svh[:, k, :],
                                       [("wslot", slh)] + [("hT", k, n) for k in range(8)], 0, 512)
                        S.op("act", [("ps", b)], [("B3", i, half), "R2c"],
                             lambda e: e.activation(out=B3[:, i, half * 512:(half + 1) * 512], in_=ps[b][:, 0:512],
                                                    func=AF.Gelu_apprx_tanh))
                        if issamp:
                            S.op("act", [("ps", b)], [("vtm", vsamp)] if half == 0 else [("vtmx", vsamp)],
                                 lambda e: e.activation(out=vtm[vsamp][:, half * 512:(half + 1) * 512],
                                                        in_=ps[b][:, 0:512], func=AF.Gelu_apprx_tanh))
                            S.op("dve", [("vtm", vsamp), ("vtmx", vsamp)], [("vstats", i, half)],
                                 lambda e: e.bn_stats(out=vstats[:, i, half, :],
                                                      in_=vtm[vsamp][:, half * 512:(half + 1) * 512]))
                        else:
                            S.op("dve", [("B3", i, half)], [("vstats", i, half)],
                                 lambda e: e.bn_stats(out=vstats[:, i, half, :], in_=B3[:, i, half * 512:(half + 1) * 512]))
                def v_finish():
                    for i in range(NT):
                        S.op("dve", [("vstats", i, 0), ("vstats", i, 1)], [("vmv", i)],
                             lambda e: e.bn_aggr(out=vmv[:, i, :], in_=vstats[:, i, :, :].rearrange("p a s -> p (a s)")))
                    S.op("act", [("vmv", i) for i in range(NT)], ["vrs"],
                         lambda e: e.activation(out=vrs[:, 0:NT], in_=vmv[:, 0:NT, 1], func=AF.Ln, bias=EPS, scale=1.0))
                    S.op("act", ["vrs"], ["vrs"],
                         lambda e: e.activation(out=vrs[:, 0:NT], in_=vrs[:, 0:NT], func=AF.Exp, scale=-0.5))
                    S.op("dve", ["vrs"] + [("vmv", i) for i in range(NT)], ["vnm"],
                         lambda e: e.scalar_tensor_tensor(out=vnm[:, 0:NT], in0=vmv[:, 0:NT, 0], scalar=-1.0, in1=vrs[:, 0:NT],
                                                          op0=ALU.mult, op1=ALU.mult))
                    for i in range(NT):
                        issamp = samp and i == NT - 1
                        if issamp:
                            v = vsamp
                            S.op("act", [("vtm", v), ("vtmx", v), "vrs", "vnm"], [("B3", i, 0), ("B3", i, 1), "R2c"],
                                 lambda e: e.activation(out=B3[:, i, :], in_=vtm[v][:, :], func=AF.Identity,
                                                        scale=vrs[:, i:i + 1], bias=vnm[:, i:i + 1]))
                            S.op("act", [("vtm", v), ("vtmx", v), "vrs", "vnm"], [("vtm", v), ("vtmx", v)],
                                 lambda e: e.activation(out=vtm[v][:, :], in_=vtm[v][:, :], func=AF.Identity,
                                                        scale=vrs[:, i:i + 1], bias=vnm[:, i:i + 1]))
                            S.op("dve", [("vtm", v), ("vtmx", v), "gb32"], [("vtm", v), ("vtmx", v)],
                                 lambda e: e.tensor_tensor(out=vtm[v][:, :], in0=vtm[v][:, :], in1=gb32[:], op=ALU.mult))
                            S.op("dve", [("vtm", v), ("vtmx", v), "bb32"], [("vtm", v), ("vtmx", v)],
                                 lambda e: e.tensor_tensor(out=vtm[v][:, :], in0=vtm[v][:, :], in1=bb32[:], op=ALU.add))
                            S.dma([("vtm", v), ("vtmx", v)], [("svs", l)], svs[l], vtm[v][:, :])
                        else:
                            S.op("dve", [("B3", i, 0), ("B3", i, 1), "vrs", "vnm"], [("B3", i, 0), ("B3", i, 1), "R2c"],
                                 lambda e: e.tensor_scalar(out=B3[:, i, :], in0=B3[:, i, :], scalar1=vrs[:, i:i + 1],
                                                           scalar2=vnm[:, i:i + 1], op0=ALU.mult, op1=ALU.add))

                for n, (n0, N) in enumerate(subs):
                    for g in (3, 2, 1, 0):
                        b = proj_group(1, lambda k: wpg[:, g, :], lambda k: dT[:, g, n0:n0 + N],
                                       ["wpg", ("dT", g, n)], n0, N)
                        S.op("act", [("ps", b), "pcB"], [("pT", g, n)],
                             lambda e, b=b, g=g: e.activation(out=pT[:, g, n0:n0 + N], in_=ps[b][:, 0:N], func=AF.Copy,
                                                              scale=pscol(l, g)))

                v_finish()

                sl_ba, sv_ba = load_slab(wview(w_ba[l], 0, 1024), 4, 1024)
                for half in range(2):
                    sl, sv = load_slab(wview(w_in_l, C_GA + 512 * half, 512), 8, 512)
                    for jj in range(4):
                        j = half * 4 + jj
                        for n, (n0, N) in enumerate(subs):
                            b = proj_group(8, lambda k: sv[:, k, jj * 128:(jj + 1) * 128], lambda k: hT[:, k, n0:n0 + N],
                                           [("wslot", sl)] + [("hT", k, n) for k in range(8)], n0, N)
                            q = nxt("sg", 3)
                            S.op("act", [("ps", b)], [("sg", q)],
                                 lambda e, b=b, q=q: e.activation(out=sgring[q][:, 0:N], in_=ps[b][:, 0:N],
                                                                  func=AF.Sigmoid))
                            b2 = proj_group(4, lambda k: sv_ba[:, k, j * 128:(j + 1) * 128],
                                            lambda k: pT[:, k, n0:n0 + N],
                                            [("wslot", sl_ba)] + [("pT", k, n) for k in range(4)], n0, N)
                            S.op("dve", [("ps", b2), ("sg", q)], [("B2", j, n), "R2b"],
                                 lambda e, b2=b2, q=q, j=j: e.tensor_tensor(out=B2[:, j, n0:n0 + N], in0=ps[b2][:, 0:N],
                                                                            in1=sgring[q][:, 0:N], op=ALU.mult))

                for half in range(2):
                    sl, sv = load_slab(wview(w_in_l, C_GB + 512 * half, 512), 8, 512)
                    for hh in range(4):
                        h = half * 4 + hh
                        for n, (n0, N) in enumerate(subs):
                            tiles = list(range(n0 // 128, (n0 + N) // 128))
                            b = bank()

                            def emit(e, tiles=tiles, b=b, h=h):
                                last = None
                                for ti, i in enumerate(tiles):
                                    Wt = WsT if (samp and i == NT - 1) else WmT
                                    last = e.matmul(ps[b][:, ti * 128:(ti + 1) * 128], lhsT=B3[:, i, h * 128:(h + 1) * 128],
                                                    rhs=Wt[:, h, :], start=True, stop=True)
                                return last
                            S.op("pe", [("B3", i, hf) for i in tiles for hf in range(2)] + ["R2c", ("WmT", (h // 4) * 4), ("WsT", (h // 4) * 4)],
                                 [("ps", b)], emit)
                            npt = len(tiles) - (1 if (samp and tiles[-1] == NT - 1) else 0)
                            if npt > 0:
                                S.op("dve", [("ps", b), ("Cm", (h // 4) * 4), "pcB"], [("ps", b)],
                                     lambda e, b=b, h=h, npt=npt: e.scalar_tensor_tensor(
                                         out=ps[b][:, 0:npt * 128].rearrange("p (a t) -> p a t", a=npt),
                                         in0=ps[b][:, 0:npt * 128].rearrange("p (a t) -> p a t", a=npt),
                                         scalar=lngcol(l, h),
                                         in1=Cm[:, h, :].unsqueeze(1).broadcast_to([128, npt, 128]),
                                         op0=ALU.mult, op1=ALU.add))
                            if npt < len(tiles):
                                S.op("dve", [("ps", b), ("Cs", (h // 4) * 4), "pcB"], [("ps", b)],
                                     lambda e, b=b, h=h, npt=npt: e.scalar_tensor_tensor(
                                         out=ps[b][:, npt * 128:(npt + 1) * 128],
                                         in0=ps[b][:, npt * 128:(npt + 1) * 128], scalar=lngcol(l, h),
                                         in1=Cs[:, h, :], op0=ALU.mult, op1=ALU.add))
                            S.op("dve", [("ps", b), ("B0", h, n)], [("B0", h, n), "R2a"],
                                 lambda e, b=b, h=h: e.tensor_tensor(out=B0[:, h, n0:n0 + N], in0=B0[:, h, n0:n0 + N],
                                                                     in1=ps[b][:, 0:N], op=ALU.mult))
                        for n, (n0, N) in enumerate(subs):
                            b = proj_group(8, lambda k: sv[:, k, hh * 128:(hh + 1) * 128], lambda k: hT[:, k, n0:n0 + N],
                                           [("wslot", sl)] + [("hT", k, n) for k in range(8)], n0, N)
                            S.op("act", [("ps", b)], [("B1", h, n), "R2d"],
                                 lambda e, b=b, h=h: e.activation(out=B1[:, h, n0:n0 + N], in_=ps[b][:, 0:N],
                                                                  func=AF.Sigmoid))

                for half in range(2):
                    sl2, sv2 = load_slab(wview(w_bb[l], 512 * half, 512), 8, 512)
                    for jj in range(4):
                        j = half * 4 + jj
                        for n, (n0, N) in enumerate(subs):
                            b2 = proj_group(8, lambda k: sv2[:, k, jj * 128:(jj + 1) * 128],
                                            lambda k: B0[:, k, n0:n0 + N],
                                            [("wslot", sl2), "R2a"] + [("B0", k, n) for k in range(8)], n0, N)
                            S.op("dve", [("ps", b2), ("B1", j, n)], [("ps", b2)],
                                 lambda e, b2=b2, j=j: e.tensor_tensor(out=ps[b2][:, 0:N], in0=ps[b2][:, 0:N],
                                                                       in1=B1[:, j, n0:n0 + N], op=ALU.mult))
                            S.op("dve", [("ps", b2), ("B2", j, n)], [("B2", j, n), "R2b"],
                                 lambda e, b2=b2, j=j: e.tensor_tensor(out=B2[:, j, n0:n0 + N], in0=B2[:, j, n0:n0 + N],
                                                                       in1=ps[b2][:, 0:N], op=ALU.add))

                slabs = {}

                def wout_slab(j, slabs=slabs, l=l):
                    half = j // 4
                    if half not in slabs:
                        slabs[half] = load_slab(wview(w_out[l], 512 * half, 512), 8, 512)
                    sl, sv = slabs[half]
                    return sl, sv, (j % 4) * 128
                out_proj_with_stats(8, B2, "B2", ["R2b"], wout_slab, subs, tail=8,
                                    mid_cb=lambda: post_norm_sub(l, 1, 0, subs[0][0], subs[0][1]))
                ffn_h(l, 0, subs[0][0], subs[0][1])
                ffn_boundary_sub(l, 1, subs[1][0], subs[1][1])
                ffn_stats = ffn_stats_fn(l, subs)

                nx = next_layer(bi, l)
                if nx is not None:
                    all_param_dmas(nx[1], BLOCKS[nx[0]][2])

                RK = ["R2a", "R2b", "R2c", "R2d"]
                def up_group(sl, sv, s8, jj, n, n0, N):
                    j = s8 * 4 + jj
                    b = proj_group(8, lambda k: sv[:, k, jj * 128:(jj + 1) * 128], lambda k: hT[:, k, n0:n0 + N],
                                   [("wslot", sl)] + [("hT", k, n) for k in range(8)], n0, N)
                    S.op("act", [("ps", b)], [("fT", j, n), RK[j // 8]],
                         lambda e: e.activation(out=fT[:, j, n0:n0 + N], in_=ps[b][:, 0:N], func=AF.Relu))
                    S.op("dve", [("fT", j, n)], [("fT", j, n), RK[j // 8]],
                         lambda e: e.tensor_tensor(out=fT[:, j, n0:n0 + N], in0=fT[:, j, n0:n0 + N],
                                                   in1=fT[:, j, n0:n0 + N], op=ALU.mult))
                up01 = [load_slab(wview(w_up[l], 512 * s8, 512), 8, 512) for s8 in range(3)]
                def stats_step(n):
                    if ffn_stats[n]:
                        sq_, mm_ = ffn_stats[n].pop(0)
                        if sq_ is not None:
                            sq_()
                        if mm_ is not None:
                            mm_()
                for n, (n0, N) in enumerate(subs):
                    for s8 in range(3):
                        for jj in range(4):
                            up_group(up01[s8][0], up01[s8][1], s8, jj, n, n0, N)
                            stats_step(n)
                    while ffn_stats[n]:
                        stats_step(n)
                for s8 in range(3, 8):
                    sl, sv = load_slab(wview(w_up[l], 512 * s8, 512), 8, 512)
                    for jj in range(4):
                        for n, (n0, N) in enumerate(subs):
                            up_group(sl, sv, s8, jj, n, n0, N)
                    if nx is not None and s8 in (3, 5):
                        prep_compute(nx[1], BLOCKS[nx[0]][2], phases=(0,) if s8 == 3 else (1,))

                if nx is not None:
                    prep_compute(nx[1], BLOCKS[nx[0]][2], phases=(2,))

                dslabs = {}

                def wdown_slab(j, l=l, dslabs=dslabs):
                    if j not in dslabs:
                        dslabs[j] = load_slab(wview(w_down[l], 128 * j, 128), 32, 128)
                    return dslabs[j][0], dslabs[j][1], 0
                has_next = l + 1 < DEPTH

                def mid_down():
                    post_norm_sub(l, 3, 0, subs[0][0], subs[0][1], corr=True)
                    if has_next:
                        pre_norm_sq(0, subs[0][0], subs[0][1])

                def late_down():
                    if has_next:
                        pre_norm_fin(l + 1, 0, 0, subs[0][0], subs[0][1])
                out_proj_with_stats(32, fT, "fT", RK, wdown_slab, subs, tail=3, mid_cb=mid_down, late_cb=late_down)
                post_norm_sub(l, 3, 1, subs[1][0], subs[1][1], corr=True)
                if has_next:
                    pre_norm_sq(1, subs[1][0], subs[1][1])
                    deferred_fin.append(lambda l=l: pre_norm_fin(l + 1, 0, 1, subs[1][0], subs[1][1]))

            for i in range(NT):
                v = nxt("vtm", 2)
                n = tile_sub(i)
                for half in range(2):
                    b = bank()
                    for cc in range(4):
                        c = half * 4 + cc
                        S.op("pe", [("xT", c, n), "ident"], [("ps", b)] if cc == 0 else [("psx", b, cc)],
                             lambda e, b=b, c=c, cc=cc, i=i: e.transpose(out=ps[b][:, cc * 128:(cc + 1) * 128],
                                                                         in_=xT[:, c, i * 128:(i + 1) * 128],
                                                                         identity=ident[:]))
                    copy_op(alt_eng(), [("ps", b), ("psx", b, 1), ("psx", b, 2), ("psx", b, 3)],
                            [("vtm", v)] if half == 0 else [("vtmx", v)],
                            vtm[v][:, half * 512:(half + 1) * 512], ps[b][:, :])
                dst = ys if (samp and i == NT - 1) else yp[p0 + 128 * i: p0 + 128 * (i + 1), :]
                S.dma([("vtm", v), ("vtmx", v)], [("yout", bi, i)], dst, vtm[v][:])
        S.finish()
    return nc


_CONSTS = None


def _consts():
    global _CONSTS
    if _CONSTS is None:
        ident = np.eye(128, dtype=np.float32)
        s = np.arange(128)
        tril = (s[:, None] <= s[None, :]).astype(np.float32)
        bmask = ((s[:, None] // 8) == (s[None, :] // 8)).astype(np.float32)
        rep = np.zeros((128, 128), np.float32)
        rep[s % 8, s] = 1.0
        inv = np.zeros((4, 15), np.float32)
        for g, w in enumerate((2, 4, 8, 16)):
            for pos in range(15):
                inv[g, pos] = 1.0 / min(w, pos + 1)
        invcnt = np.ascontiguousarray(np.broadcast_to(inv.reshape(1, 60), (128, 60)))
        _CONSTS = dict(c_ident=ident, c_tril=tril, c_bmask=bmask, c_rep=rep, c_invcnt=invcnt)
    return _CONSTS


_NC = None


def kernel(x_prompt, x_sample, state_pool, w_in, w_pool_grp, pool_scale, w_spatial, b_spatial,
           ln_v_g, ln_v_b, w_branch_a, w_branch_b, w_out, w_up, w_down,
           g_pre_mix, g_post_mix, g_pre_ffn, g_post_ffn):
    global _NC
    f = lambda a: np.ascontiguousarray(np.asarray(a, dtype=np.float32))
    if _NC is None:
        _NC = build()
    nc = _NC
    x_prompt = f(x_prompt); x_sample = f(x_sample); state_pool = f(state_pool)
    shared = dict(w_in=f(w_in), w_pg=f(w_pool_grp), pool_scale=f(pool_scale), w_sp=f(w_spatial),
                  b_sp=f(b_spatial).reshape(DEPTH, 1024), ln_g=f(ln_v_g), ln_b=f(ln_v_b), w_ba=f(w_branch_a),
                  w_bb=f(w_branch_b), w_out=f(w_out), w_up=f(w_up), w_down=f(w_down), g_pre_mix=f(g_pre_mix),
                  g_post_mix=f(g_post_mix), g_pre_ffn=f(g_pre_ffn), g_post_ffn=f(g_post_ffn))
    shared.update(_consts())
    in_maps = []
    for c in range(NCORES):
        m = dict(shared)
        m["xp"] = x_prompt[c]
        m["xs"] = np.ascontiguousarray(x_sample[16 * c:16 * (c + 1)].reshape(128, D))
        m["sp"] = np.ascontiguousarray(state_pool[:, 16 * c:16 * (c + 1)].reshape(DEPTH, 240, 512))
        in_maps.append(m)
    res = run_bass_kernel_spmd(nc, in_maps, core_ids=list(range(NCORES)))
    r = res.results
    y_prompt = np.stack([r[c]["yp"] for c in range(NCORES)], axis=0)
    y_sample = np.concatenate([r[c]["ys"].reshape(16, 8, D) for c in range(NCORES)], axis=0)
    spp = np.stack([r[c]["spp"] for c in range(NCORES)], axis=1)
    sps = np.concatenate([r[c]["sps"] for c in range(NCORES)], axis=1)
    svs = np.concatenate([r[c]["svs"].reshape(DEPTH, 16, 8, D) for c in range(NCORES)], axis=1)
    return (y_prompt.astype(np.float32), y_sample.astype(np.float32), spp.astype(np.float32),
            sps.astype(np.float32), svs.astype(np.float32))
```
